# Optimizing a Trainium2 kernel written in Bass

```python
import jax, jax.numpy as jnp
from jax import lax
import numpy as np

D_MODEL = 1024
BATCH = 4
SEQ = 8192
DEPTH = 2

CHUNK = 64
Q_BLOCK = 128
FOX_HEADS = 8
FOX_HEAD_DIM = 64
FOX_WIDTH = FOX_HEADS * FOX_HEAD_DIM
CONV_WIDTH = 512
CONV_K = 3
EVEN_IN = 3 * FOX_WIDTH + FOX_HEADS + 3 * CONV_WIDTH
EVEN_MIX = FOX_WIDTH + CONV_WIDTH
GMLP_BLOCK = 128
GMLP_GROUPS = 8
GMLP_WIDTH = D_MODEL
GMLP_GROUP_DIM = GMLP_WIDTH // GMLP_GROUPS
FFN_HIDDEN = -(-8 * D_MODEL // (3 * 256)) * 256
ALPHA = (2.0 * DEPTH) ** 0.25
BETA = (8.0 * DEPTH) ** -0.25
N_EVEN = (DEPTH + 1) // 2
N_ODD = DEPTH // 2
LN_EPS = 1e-5

kernel_name = "fox_shortconv_gmlp_deepnorm_trunk"


def layer_norm(x, g, b):
    xf = x.astype(jnp.float32)
    mu = jnp.mean(xf, axis=-1, keepdims=True)
    var = jnp.mean(jnp.square(xf - mu), axis=-1, keepdims=True)
    return ((xf - mu) * lax.rsqrt(var + LN_EPS) * g + b).astype(x.dtype)


def forgetting_attention(q, k, v, log_f):
    bsz, s_len, h, dh = q.shape
    nb = s_len // Q_BLOCK
    c = jnp.cumsum(log_f, axis=1).transpose(0, 2, 1)
    kh = k.transpose(0, 2, 1, 3)
    vh = v.transpose(0, 2, 1, 3)
    qb = q.reshape(bsz, nb, Q_BLOCK, h, dh).transpose(1, 0, 3, 2, 4)
    cb = c.reshape(bsz, h, nb, Q_BLOCK).transpose(2, 0, 1, 3)
    pos = jnp.arange(s_len)
    posb = pos.reshape(nb, Q_BLOCK)
    scale = dh ** -0.5

    def block(args):
        q_blk, c_blk, p_blk = args
        s = jnp.einsum('bhqd,bhkd->bhqk', q_blk, kh,
                       preferred_element_type=jnp.float32) * scale
        s = s + c_blk[..., :, None] - c[..., None, :]
        s = jnp.where(p_blk[:, None] >= pos[None, :], s, -jnp.inf)
        p = jax.nn.softmax(s, axis=-1)
        return jnp.einsum('bhqk,bhkd->bhqd', p.astype(vh.dtype), vh)

    o = lax.map(block, (qb, cb, posb))
    return o.transpose(1, 0, 3, 2, 4).reshape(bsz, s_len, h * dh)


def short_conv_mixer(h, b_gate, c_gate, conv_w):
    s_len = h.shape[1]
    z = c_gate * h
    zp = jnp.pad(z, ((0, 0), (CONV_K - 1, 0), (0, 0)))
    y = conv_w[0] * zp[:, 0:s_len]
    for i in range(1, CONV_K):
        y = y + conv_w[i] * zp[:, i:i + s_len]
    return b_gate * y


def fox_conv_mixer(x, w_in, b_f, conv_w, w_out):
    bsz, s_len, _ = x.shape
    proj = x @ w_in
    cuts = np.cumsum([FOX_WIDTH, FOX_WIDTH, FOX_WIDTH, FOX_HEADS, CONV_WIDTH, CONV_WIDTH]).tolist()
    q, k, v, f_logit, b_gate, c_gate, h = jnp.split(proj, cuts, axis=-1)
    log_f = jax.nn.log_sigmoid((f_logit + b_f).astype(jnp.float32))
    heads = (bsz, s_len, FOX_HEADS, FOX_HEAD_DIM)
    attn = forgetting_attention(q.reshape(heads), k.reshape(heads), v.reshape(heads), log_f)
    conv = short_conv_mixer(h, b_gate, c_gate, conv_w)
    return jnp.concatenate([attn.astype(x.dtype), conv], axis=-1) @ w_out


def gmlp_mixer(x, w_in, v_ln_g, v_ln_b, w_s, b_s, w_out):
    bsz, s_len, _ = x.shape
    uv = jax.nn.gelu(x @ w_in, approximate=False)
    u, v = jnp.split(uv, 2, axis=-1)
    v = layer_norm(v, v_ln_g, v_ln_b)
    nc = s_len // GMLP_BLOCK
    vb = v.reshape(bsz, nc, GMLP_BLOCK, GMLP_GROUPS, GMLP_GROUP_DIM)
    chunk_id = jnp.arange(GMLP_BLOCK) // CHUNK
    mask = chunk_id[None, :] <= chunk_id[:, None]
    w = jnp.where(mask[None], w_s, jnp.zeros((), w_s.dtype))
    s = jnp.einsum('gij,bcjgd->bcigd', w, vb) + b_s.T[None, None, :, :, None]
    return (u * s.reshape(bsz, s_len, GMLP_WIDTH)) @ w_out


def swiglu(x, w_in, w_out):
    gate, up = jnp.split(x @ w_in, 2, axis=-1)
    return (jax.nn.silu(gate) * up) @ w_out


def setup_inputs(seed: int = 0) -> dict:
    key = jax.random.key(seed)
    ks = jax.random.split(key, 20)
    nrm = jax.random.normal
    f32 = jnp.float32
    return {
        "x": nrm(ks[0], (BATCH, SEQ, D_MODEL), f32),
        "even_w_in": nrm(ks[1], (N_EVEN, D_MODEL, EVEN_IN), f32) * D_MODEL ** -0.5,
        "even_b_f": jax.random.uniform(ks[2], (N_EVEN, FOX_HEADS), f32, 1.0, 5.0),
        "even_conv_w": nrm(ks[3], (N_EVEN, CONV_K, CONV_WIDTH), f32) * CONV_K ** -0.5,
        "even_w_out": nrm(ks[4], (N_EVEN, EVEN_MIX, D_MODEL), f32) * (EVEN_MIX ** -0.5 * BETA),
        "odd_w_in": nrm(ks[5], (N_ODD, D_MODEL, 2 * GMLP_WIDTH), f32) * D_MODEL ** -0.5,
        "odd_v_ln_g": 1.0 + 0.1 * nrm(ks[6], (N_ODD, GMLP_WIDTH), f32),
        "odd_v_ln_b": 0.1 * nrm(ks[7], (N_ODD, GMLP_WIDTH), f32),
        "odd_w_s": nrm(ks[8], (N_ODD, GMLP_GROUPS, GMLP_BLOCK, GMLP_BLOCK), f32) * GMLP_BLOCK ** -0.5,
        "odd_b_s": 1.0 + 0.1 * nrm(ks[9], (N_ODD, GMLP_GROUPS, GMLP_BLOCK), f32),
        "odd_w_out": nrm(ks[10], (N_ODD, GMLP_WIDTH, D_MODEL), f32) * (GMLP_WIDTH ** -0.5 * BETA),
        "mix_ln_g": 1.0 + 0.1 * nrm(ks[11], (DEPTH, D_MODEL), f32),
        "mix_ln_b": 0.1 * nrm(ks[12], (DEPTH, D_MODEL), f32),
        "ffn_w_in": nrm(ks[13], (DEPTH, D_MODEL, 2 * FFN_HIDDEN), f32) * D_MODEL ** -0.5,
        "ffn_w_out": nrm(ks[14], (DEPTH, FFN_HIDDEN, D_MODEL), f32) * (FFN_HIDDEN ** -0.5 * BETA),
        "ffn_ln_g": 1.0 + 0.1 * nrm(ks[15], (DEPTH, D_MODEL), f32),
        "ffn_ln_b": 0.1 * nrm(ks[16], (DEPTH, D_MODEL), f32),
    }


def reference(x, even_w_in, even_b_f, even_conv_w, even_w_out, odd_w_in, odd_v_ln_g,
              odd_v_ln_b, odd_w_s, odd_b_s, odd_w_out, mix_ln_g, mix_ln_b, ffn_w_in,
              ffn_w_out, ffn_ln_g, ffn_ln_b):
    for layer in range(DEPTH):
        i = layer // 2
        if layer % 2 == 0:
            m = fox_conv_mixer(x, even_w_in[i], even_b_f[i], even_conv_w[i], even_w_out[i])
        else:
            m = gmlp_mixer(x, odd_w_in[i], odd_v_ln_g[i], odd_v_ln_b[i], odd_w_s[i],
                           odd_b_s[i], odd_w_out[i])
        x = layer_norm(ALPHA * x + m, mix_ln_g[layer], mix_ln_b[layer])
        x = layer_norm(ALPHA * x + swiglu(x, ffn_w_in[layer], ffn_w_out[layer]),
                       ffn_ln_g[layer], ffn_ln_b[layer])
    return x
```

```python
import numpy as np
from contextlib import ExitStack
import concourse.bass as bass
import concourse.mybir as mybir
from concourse.bass_utils import run_bass_kernel_spmd

F32 = mybir.dt.float32
BF16 = mybir.dt.bfloat16
AF = mybir.ActivationFunctionType
ALU = mybir.AluOpType

D = 1024
KC = 8
SEQ = 8192
NOWN = 4096
H = 8
FH = 2816
NJ = 22
ALPHA = float((2.0 * 2) ** 0.25)
EPS = 1e-5
NEG = -30000.0
ENGS = ['pe', 'act', 'dve', 'pool', 'sp']
EPOCH = 20000


class Op:
    __slots__ = ('eng', 'fns', 'is_dma', 'dkey', 'dval', 'idx', 'deps', 'needed', 'sig')


class Sched:
    def __init__(self, nc):
        self.nc = nc
        self.ops = {e: [] for e in ENGS}
        self.last_writer = {}
        self.readers = {}
        self.dma_cum = {}
        self.bar_deps = []
        self.pending_dma = []
        self.enabled = True

    def barrier(self):
        deps = []
        for e in ENGS:
            for op in reversed(self.ops[e]):
                if not op.is_dma:
                    deps.append(op)
                    break
        last = {}
        for op in self.pending_dma:
            last[op.dkey] = op
        deps.extend(last.values())
        self.pending_dma = list(last.values())
        self.bar_deps = deps

    def add(self, eng, fns, reads=(), writes=(), dkey=None):
        if not self.enabled:
            return None
        if not isinstance(fns, (list, tuple)):
            fns = [fns]
        op = Op()
        op.eng = eng
        op.fns = list(fns)
        op.is_dma = dkey is not None
        op.dkey = dkey
        op.needed = False
        op.sig = None
        op.dval = 0
        if op.is_dma:
            self.dma_cum[dkey] = self.dma_cum.get(dkey, 0) + 16 * len(op.fns)
            op.dval = self.dma_cum[dkey]
            self.pending_dma.append(op)
        deps = {id(o): o for o in self.bar_deps}
        for r in reads:
            w = self.last_writer.get(r)
            if w is not None:
                deps[id(w)] = w
        for k in writes:
            w = self.last_writer.get(k)
            if w is not None:
                deps[id(w)] = w
            rd = self.readers.get(k)
            if rd:
                for o in rd['eng'].values():
                    deps[id(o)] = o
                for o in rd['dma']:
                    deps[id(o)] = o
        deps.pop(id(op), None)
        op.deps = list(deps.values())
        for r in reads:
            rd = self.readers.setdefault(r, {'eng': {}, 'dma': []})
            if op.is_dma:
                rd['dma'].append(op)
            else:
                rd['eng'][eng] = op
        for k in writes:
            self.last_writer[k] = op
            self.readers[k] = {'eng': {}, 'dma': []}
        op.idx = len(self.ops[eng])
        self.ops[eng].append(op)
        return op

    def emit(self, stack):
        nc = self.nc
        for e in ENGS:
            for op in self.ops[e]:
                for d in op.deps:
                    if d.is_dma:
                        continue
                    if d.eng == op.eng and d.eng == 'pe':
                        continue
                    d.needed = True
        eng_sems = {}
        for e in ENGS:
            cnt = 0
            for op in self.ops[e]:
                if op.is_dma:
                    continue
                if op.needed:
                    op.sig = ((e, cnt // EPOCH), cnt % EPOCH + 1)
                    cnt += 1
            for ep in range((cnt + EPOCH - 1) // EPOCH):
                eng_sems[(e, ep)] = stack.enter_context(nc.semaphore(f"s_{e}_{ep}"))
        dma_sems = {}
        for k in self.dma_cum:
            dma_sems[k] = stack.enter_context(nc.semaphore(f"d_{len(dma_sems)}"))
        self.n_sems = len(eng_sems) + len(dma_sems)
        block = stack.enter_context(nc.Block())

        def run_engine(e, eng):
            waited = {}
            for op in self.ops[e]:
                w = {}
                for d in op.deps:
                    if d.is_dma:
                        key, val = ('d', d.dkey), d.dval
                    else:
                        if d.eng == e and e == 'pe':
                            continue
                        key, val = d.sig
                    if w.get(key, 0) < val:
                        w[key] = val
                for key, val in w.items():
                    if waited.get(key, 0) >= val:
                        continue
                    waited[key] = val
                    sem = dma_sems[key[1]] if key[0] == 'd' else eng_sems[key]
                    eng.wait_ge(sem, val)
                ins = None
                for f in op.fns:
                    ins = f(eng)
                    if op.is_dma:
                        ins.then_inc(dma_sems[op.dkey], 16)
                if (not op.is_dma) and op.sig is not None:
                    ins.then_inc(eng_sems[op.sig[0]], 1)
            if e == 'sp':
                for k, v in self.dma_cum.items():
                    if waited.get(('d', k), 0) < v:
                        eng.wait_ge(dma_sems[k], v)

        @block.tensor
        def _(eng):
            run_engine('pe', eng)

        @block.scalar
        def _(eng):
            run_engine('act', eng)

        @block.vector
        def _(eng):
            run_engine('dve', eng)

        @block.gpsimd
        def _(eng):
            run_engine('pool', eng)

        @block.sync
        def _(eng):
            run_engine('sp', eng)


class Arena:
    def __init__(self, ap, base, limit):
        self.ap = ap
        self.off = base
        self.limit = limit

    def f32(self, n, parts=128):
        n8 = (n + 7) // 8 * 8
        a = self.ap[0:parts, self.off:self.off + n]
        self.off += n8
        assert self.off <= self.limit, (self.off, self.limit)
        return a

    def bf16(self, n, parts=128):
        w = (n + 1) // 2
        return self.f32(w, parts).bitcast(BF16)[:, 0:n]


def mm(out, lhsT, rhs, start, stop):
    return lambda e: e.matmul(out, lhsT=lhsT, rhs=rhs, start=start, stop=stop)


def build(debug=False, stop_after=99):
    nc = bass.Bass("TRN2", target_bir_lowering=False)

    def din(name, shape):
        return nc.dram_tensor(name, shape, F32, kind="ExternalInput").ap()

    def scratch(name, shape, dt):
        return nc.dram_tensor(name, shape, dt, kind=("ExternalOutput" if debug else "Internal")).ap()

    xfull = din("xfull", [SEQ, D])
    xown = din("xown", [NOWN, D])
    xhalo = din("xhalo", [64, D])
    maskd = din("maskd", [128, 256])
    selv = din("selv", [8, 2])
    triseg = din("triseg", [128, 128])
    lngc_d = din("lngc", [128, 32])
    lnbc_d = din("lnbc", [128, 32])
    w_in0 = din("w_in0", [D, 3080])
    b_f = din("b_f", [8, 1])
    cwl = din("cwl", [128, 12])
    w_out0 = din("w_out0", [D, D])
    w_in1 = din("w_in1", [D, 2048])
    vln_g = din("vln_g", [1, D])
    vln_b = din("vln_b", [1, D])
    wsT_d = din("wsT", [128, 1024])
    b_s = din("b_s", [1, D])
    w_out1 = din("w_out1", [D, D])
    mix_g = din("mix_g", [2, D])
    mix_b = din("mix_b", [2, D])
    ffn_w1 = din("ffn_w1", [2, D, 2 * FH])
    ffn_w2 = din("ffn_w2", [2, FH, D])
    ffn_g = din("ffn_g", [2, D])
    ffn_b = din("ffn_b", [2, D])
    out = nc.dram_tensor("out", [NOWN, D], F32, kind="ExternalOutput").ap()

    KT_s = scratch("KT_s", [H, 70, SEQ], BF16)
    V_s = scratch("V_s", [H, 128, 64, 128], BF16)
    QT_s = scratch("QT_s", [H, 70, NOWN], BF16)
    mixT_s = scratch("mixT_s", [D, NOWN], BF16)
    x1_s = scratch("x1_s", [NOWN, D], F32)
    xT_s = scratch("xT_s", [D, NOWN], BF16)
    aT_s = scratch("aT_s", [FH, NOWN], BF16)
    x2_s = scratch("x2_s", [NOWN, D], F32)
    gT_s = scratch("gT_s", [D, NOWN], BF16)
    x3_s = scratch("x3_s", [NOWN, D], F32)
    f_s = scratch("f_s", [8, SEQ], F32)
    ck_s = scratch("ck_s", [3, 128, 512], BF16)
    cq_s = scratch("cq_s", [3, 128, 256], BF16)

    st = ExitStack()
    with st:
        AW = 45056
        arena_t = st.enter_context(nc.sbuf_tensor("arena", [128, AW], F32))
        bpairs = [st.enter_context(nc.psum_tensor(f"bpair{i}", [128, 1024], F32)) for i in range(4)]
        banks = [bpairs[i // 2][:, (i % 2) * 512:(i % 2 + 1) * 512] for i in range(8)]
        S = Sched(nc)
        bank_ctr = [0]

        def nb():
            i = bank_ctr[0] % 8
            bank_ctr[0] += 1
            return banks[i], ('ps', i)

        P = Arena(arena_t, 0, 4096)
        identf = P.f32(128)
        identb = P.bf16(128)
        maskf = P.f32(256)
        maskb = P.bf16(256)
        onesb = P.bf16(128)
        cw = P.f32(12)
        bft = P.f32(1, 8)
        selt = P.f32(2, 8)
        lnscr = [dict(st=P.f32(12), mv=P.f32(2), sd=P.f32(1), rs=P.f32(1), nb=P.f32(1)) for _ in range(4)]
        lngc = P.f32(32)
        lnbc = P.f32(32)
        epst = P.f32(1)
        ones8p = P.bf16(1024, 8)
        ln_ctr = [0]
        PBASE = P.off

        S.add('pool', lambda e: e.memset(identf, 0.0), writes=['identf'])
        S.add('pool', lambda e: e.affine_select(out=identf, in_=identf, pattern=[[-1, 128]], compare_op=ALU.not_equal,
                                                fill=1.0, base=0, channel_multiplier=1), reads=['identf'], writes=['identf'])
        S.add('dve', lambda e: e.tensor_copy(out=identb, in_=identf), reads=['identf'], writes=['identb'])
        S.add('sp', lambda e: e.dma_start(out=maskf, in_=maskd), writes=['maskf'], dkey='c0')
        S.add('dve', lambda e: e.tensor_copy(out=maskb, in_=maskf), reads=['maskf'], writes=['maskb'])
        S.add('dve', lambda e: e.memset(onesb, 1.0), writes=['onesb'])
        S.add('sp', lambda e: e.dma_start(out=cw, in_=cwl), writes=['cw'], dkey='c1')
        S.add('sp', lambda e: e.dma_start(out=bft, in_=b_f), writes=['bft'], dkey='c2')
        S.add('sp', lambda e: e.dma_start(out=selt, in_=selv), writes=['selt'], dkey='c3')
        S.add('sp', lambda e: e.dma_start(out=lngc, in_=lngc_d), writes=['lngc'], dkey='c16')
        S.add('sp', lambda e: e.dma_start(out=lnbc, in_=lnbc_d), writes=['lnbc'], dkey='c17')

        def new_arena():
            return Arena(arena_t, PBASE, AW)

        def v3(a, d1, d2):
            return a.rearrange("p (a b) -> p a b", a=d1, b=d2)

        def load_w(dst3, src2d, key, kcs, dk):
            for k in range(kcs):
                S.add('pool', lambda e, k=k: e.dma_start(out=dst3[:, k, :], in_=src2d[k * 128:(k + 1) * 128, :]),
                      writes=[(key, k)], dkey=(dk, k % 4))

        def load_bcast(dst, vec_row, key, dk):
            S.add('sp', lambda e: e.dma_start(out=dst, in_=vec_row.broadcast_to([128, D])), writes=[key], dkey=dk)

        def prefetch_ln_weights(nch, w_d, g_row, b_row):
            size = nch * 512 + 2048
            base = AW - size
            T = Arena(arena_t, base, AW)
            Wt = v3(T.bf16(nch * 1024), nch, 1024)
            g_t = T.f32(1024)
            b_t = T.f32(1024)
            todo = []
            for k in range(nch):
                todo.append(lambda k=k: S.add('pool', lambda e: e.dma_start(out=Wt[:, k, :], in_=w_d[k * 128:(k + 1) * 128, :]),
                                              writes=[('Wt', k)], dkey=('w', k % 4)))
            todo.append(lambda: load_bcast(g_t, g_row, 'g_t', 'c6'))
            todo.append(lambda: load_bcast(b_t, b_row, 'b_t', 'c7'))
            return (Wt, g_t, b_t, base, todo)

        GM_LO, GM_HI = 23552, 31744

        def prefetch_gmlp_weights():
            T = Arena(arena_t, GM_LO, GM_HI)
            Wuv = v3(T.bf16(8 * 2048), 8, 2048)
            todo = []
            for k in range(8):
                todo.append(lambda k=k: S.add('pool', lambda e: e.dma_start(out=Wuv[:, k, :], in_=w_in1[k * 128:(k + 1) * 128, :]),
                                              writes=[('Wuv', k)], dkey=('w3', k % 4)))
            return (Wuv, None, None, GM_LO, todo)

        def drain_todo(nxt, n):
            if nxt is None:
                return
            for _ in range(n):
                if nxt[4]:
                    nxt[4].pop(0)()

        def layer_norm(z, zkey, o, okey, g_t, b_t, gkey, bkey, act_norm=False, gb_eng='pool', defer_gb=False):
            sc = lnscr[ln_ctr[0] % 4]
            sk = ('lnscr', ln_ctr[0] % 4)
            ln_ctr[0] += 1
            stt, mv, sd, rs, nbt = sc['st'], sc['mv'], sc['sd'], sc['rs'], sc['nb']
            S.add('dve', [lambda e: e.bn_stats(out=stt[:, 0:6], in_=z[:, 0:512]),
                          lambda e: e.bn_stats(out=stt[:, 6:12], in_=z[:, 512:1024])], reads=[zkey], writes=[(sk, 'st')])
            S.add('dve', lambda e: e.bn_aggr(out=mv, in_=stt), reads=[(sk, 'st')], writes=[sk])
            S.add('act', lambda e: e.activation(out=sd, in_=mv[:, 1:2], func=AF.Sqrt, bias=epst[:, 0:1], scale=1.0),
                  reads=[sk, 'epst'], writes=[(sk, 'sd')])
            S.add('dve', lambda e: e.reciprocal(out=rs, in_=sd), reads=[(sk, 'sd')], writes=[(sk, 'rs')])
            if act_norm:
                S.add('dve', lambda e: e.tensor_scalar(out=nbt, in0=mv[:, 0:1], scalar1=-1.0, scalar2=rs[:, 0:1], op0=ALU.mult, op1=ALU.mult),
                      reads=[sk, (sk, 'rs')], writes=[(sk, 'nb')])
                S.add('act', lambda e: e.activation(out=z, in_=z, func=AF.Identity, bias=nbt[:, 0:1], scale=rs[:, 0:1]),
                      reads=[zkey, (sk, 'rs'), (sk, 'nb')], writes=[zkey])
            else:
                S.add('dve', lambda e: e.tensor_scalar(out=z, in0=z, scalar1=mv[:, 0:1], scalar2=rs[:, 0:1],
                                                       op0=ALU.subtract, op1=ALU.mult),
                      reads=[zkey, sk, (sk, 'rs')], writes=[zkey])
            def gain_bias():
                S.add(gb_eng, lambda e: e.tensor_tensor(out=o, in0=z, in1=g_t, op=ALU.mult), reads=[zkey, gkey], writes=[okey])
                S.add(gb_eng, lambda e: e.tensor_tensor(out=o, in0=o, in1=b_t, op=ALU.add), reads=[okey, bkey], writes=[okey])
            if defer_gb:
                return gain_bias
            gain_bias()
            return None

        S.add('dve', lambda e: e.memset(epst, EPS), writes=['epst'])

        def transposes_to_bf16(src_blk, src_keys, dstT, dst_key_fn, nblk, evac='act', extra_f32=None):
            for kc in range(KC):
                bk, bkk = nb()
                S.add('pe', [lambda e, b=b, kc=kc, bk=bk: e.transpose(out=bk[:, b * 128:(b + 1) * 128],
                                                                      in_=src_blk(b)[:, kc * 128:(kc + 1) * 128],
                                                                      identity=identf) for b in range(nblk)],
                      reads=list(src_keys) + ['identf'], writes=[bkk])
                n = nblk * 128
                if extra_f32 is not None:
                    x32, x32key = extra_f32
                    S.add('act', lambda e, bk=bk, kc=kc: e.copy(out=x32[:, kc, 0:n], in_=bk[:, 0:n]),
                          reads=[bkk], writes=[(x32key, kc)])
                    S.add('dve', lambda e, kc=kc: e.tensor_copy(out=dstT[:, kc, 0:n], in_=x32[:, kc, 0:n]),
                          reads=[(x32key, kc)], writes=[dst_key_fn(kc)])
                elif evac == 'act':
                    S.add('act', lambda e, bk=bk, kc=kc: e.copy(out=dstT[:, kc, 0:n], in_=bk[:, 0:n]),
                          reads=[bkk], writes=[dst_key_fn(kc)])
                else:
                    S.add('dve', lambda e, bk=bk, kc=kc: e.tensor_copy(out=dstT[:, kc, 0:n], in_=bk[:, 0:n]),
                          reads=[bkk], writes=[dst_key_fn(kc)])

        A = new_arena()
        Wkvf = v3(A.bf16(8 * 1032), 8, 1032)
        Wf32 = v3(A.f32(64), 8, 8)
        fT = A.f32(SEQ, 8)
        S1C_BASE = A.off
        xs = [v3(A.f32(4096), 4, 1024) for _ in range(2)]
        xT32 = v3(A.f32(4096), 8, 512)
        xTb = [v3(A.bf16(4096), 8, 512) for _ in range(2)]
        kst = [v3(A.bf16(2048), 4, 512) for _ in range(2)]
        vst = [A.bf16(8 * 4 * 128).rearrange("p (h b c) -> p h b c", h=8, b=4, c=128) for _ in range(2)]

        S.add('sp', lambda e: e.dma_start(out=Wf32, in_=w_in0[:, 1536:1544].rearrange("(k p) n -> p k n", p=128)),
              writes=['Wf32'], dkey='c4')
        S.add('dve', lambda e: e.tensor_copy(out=Wkvf[:, :, 1024:1032], in_=Wf32), reads=['Wf32'], writes=['Wfb'])
        for i in range(2):
            S.add('pool', lambda e, i=i: e.memset(vst[i][:, :, :, 64:128], 1.0), writes=[('vst1', i)])

        NT1 = 16

        def s1_load(t):
            sl = t % 2
            S.add('sp', [lambda e, b=b: e.dma_start(out=xs[sl][:, b, :], in_=xfull[t * 512 + b * 128:t * 512 + (b + 1) * 128, :]) for b in range(4)],
                  writes=[('xs', sl)], dkey=('xs', sl))

        def s1_tr(t):
            sl = t % 2
            transposes_to_bf16(lambda b: xs[sl][:, b, :], [('xs', sl)], xTb[sl], lambda kc: ('xTb', sl, kc), 4)

        def s1_f(t):
            bk, bkk = nb()
            sl = t % 2
            S.add('pe', [mm(bk[0:8, :], Wkvf[:, kc, 1024:1032], xTb[sl][:, kc, :], kc == 0, kc == 7) for kc in range(KC)],
                  reads=[('xTb', sl, kc) for kc in range(KC)] + ['Wfb'], writes=[bkk])
            S.add('dve', lambda e, bk=bk: e.tensor_scalar(out=fT[:, t * 512:(t + 1) * 512], in0=bk[0:8, :], scalar1=bft[:, 0:1], scalar2=None,
                                                          op0=ALU.add), reads=[bkk, 'bft'], writes=[('fT', t)])

        def s1_mm(t):
            sl = t % 2
            xkeys = [('xTb', sl, kc) for kc in range(KC)]
            wkeys = [('Wkvf', k) for k in range(KC)]
            for hp in range(4):
                bk, bkk = nb()
                S.add('pe', [mm(bk[:, :], Wkvf[:, kc, hp * 128:(hp + 1) * 128], xTb[sl][:, kc, :], kc == 0, kc == 7) for kc in range(KC)],
                      reads=xkeys + wkeys, writes=[bkk])
                S.add('dve', lambda e, bk=bk, hp=hp: e.tensor_copy(out=kst[sl][:, hp, :], in_=bk[:, :]), reads=[bkk], writes=[('kst', sl, hp)])
            S.add('sp', [lambda e, h=h: e.dma_start(out=KT_s[h, 0:64, t * 512:(t + 1) * 512],
                                                    in_=kst[sl][(h % 2) * 64:(h % 2 + 1) * 64, h // 2, :]) for h in range(H)],
                  reads=[('kst', sl, hp) for hp in range(4)], writes=[('KTd', h, t) for h in range(H)], dkey=('kst', sl))
            for b in range(4):
                bk, bkk = nb()
                S.add('pe', [mm(bk[:, :], xTb[sl][:, kc, b * 128:(b + 1) * 128], Wkvf[:, kc, 512:1024], kc == 0, kc == 7) for kc in range(KC)],
                      reads=xkeys + wkeys, writes=[bkk])
                S.add('act', lambda e, bk=bk, b=b: e.copy(out=vst[sl][:, :, b, 0:64], in_=bk[:, :].rearrange("p (h c) -> p h c", h=8)),
                      reads=[bkk, ('vst1', sl)], writes=[('vst', sl, b)])
            S.add('sp', lambda e: e.dma_start(out=V_s[:, :, 4 * t:4 * t + 4, :].rearrange("h p b c -> p h b c"), in_=vst[sl]),
                  reads=[('vst', sl, b) for b in range(4)], writes=[('Vd', t)], dkey=('vst', sl))

        s1_load(0)
        Wstg = v3(arena_t[:, AW - 8192:AW], 8, 1024)
        for k in range(8):
            S.add('sp', lambda e, k=k: e.dma_start(out=Wstg[:, k, :], in_=w_in0[k * 128:(k + 1) * 128, 512:1536]), writes=[('wstg', k)], dkey=('wstg', k))
            S.add('act' if k % 2 == 0 else 'dve',
                  (lambda e, k=k: e.copy(out=Wkvf[:, k, 0:1024], in_=Wstg[:, k, :])) if k % 2 == 0 else (lambda e, k=k: e.tensor_copy(out=Wkvf[:, k, 0:1024], in_=Wstg[:, k, :])),
                  reads=[('wstg', k)], writes=[('Wkvf', k)])
        s1_load(1)
        W2TOP = AW - 8192
        assert A.off <= W2TOP, (A.off, W2TOP)
        T2 = Arena(arena_t, W2TOP, AW)
        Wq = v3(T2.bf16(8 * 512), 8, 512)
        Wbch = v3(T2.bf16(8 * 1536), 8, 1536)
        w2todo = []
        for k in range(8):
            w2todo.append(lambda k=k: S.add('pool', lambda e: e.dma_start(out=Wq[:, k, :], in_=w_in0[k * 128:(k + 1) * 128, 0:512]),
                                            writes=[('Wq', k)] + [('wstg', j) for j in range(8)], dkey=('w2', k % 4)))
            w2todo.append(lambda k=k: S.add('pool', lambda e: e.dma_start(out=Wbch[:, k, :], in_=w_in0[k * 128:(k + 1) * 128, 1544:3080]),
                                            writes=[('Wbch', k)] + [('wstg', j) for j in range(8)], dkey=('w2', k % 4)))
        s1_tr(0)
        s1_f(0)
        for t in range(NT1):
            if t == 3:
                S.add('pool', lambda e: e.memset(ones8p, 1.0), writes=['ones8p'])
            if t in (4, 6, 8, 10):
                ci = (t - 4) // 2
                fl = [lambda e, r=r, q=q: e.dma_start(out=KT_s[:, 67 + r, q * 1024:(q + 1) * 1024], in_=ones8p) for r in range(3) for q in range(8)] \
                    + [lambda e, r=r, q=q: e.dma_start(out=QT_s[:, 64 + r, q * 1024:(q + 1) * 1024], in_=ones8p) for r in range(3) for q in range(4)]
                S.add('sp', fl[ci * 9:(ci + 1) * 9], reads=['ones8p'],
                      writes=[('KTc', r) for r in range(67, 70)] + [('QTc', r) for r in range(64, 67)], dkey='ones')
            if t >= 2 and w2todo:
                w2todo.pop(0)()
                if w2todo:
                    w2todo.pop(0)()
            if t + 1 < NT1:
                s1_tr(t + 1)
            if t + 2 < NT1:
                s1_load(t + 2)
            s1_mm(t)
            if t + 1 < NT1:
                s1_f(t + 1)

        S.enabled = stop_after >= 1
        S.barrier()
        S.add('sp', lambda e: e.dma_start(out=f_s, in_=fT), reads=[('fT', t) for t in range(NT1)], writes=['f_s'], dkey='c12')
        S.barrier()
        A = Arena(arena_t, 25600, W2TOP)
        f128 = A.f32(512)
        e128 = A.f32(512)
        ones5 = A.f32(512)
        cs = A.f32(512)
        tri = A.f32(128)
        tot = A.f32(1)
        offs = A.f32(1)
        rt = A.f32(512)
        khi = A.bf16(512)
        kmid = A.bf16(512)
        klo = A.bf16(512)
        qtmp = A.bf16(256)
        qhi = A.bf16(256)
        qmid = A.bf16(256)
        qlo = A.bf16(256)
        sel128 = A.f32(2)
        fkeys = [('fT', t) for t in range(NT1)]
        S.add('sp', lambda e: e.dma_start(out=f128, in_=f_s.rearrange("h (s t) -> (h s) t", s=16)), reads=['f_s'], writes=['f128'], dkey='c13')
        S.add('sp', lambda e: e.dma_start(out=tri, in_=triseg), writes=['tri'], dkey='c14')
        S.add('sp', lambda e: e.dma_start(out=sel128, in_=selv[0:1, :].broadcast_to([128, 2])), writes=['sel128'], dkey='c15')
        S.add('dve', lambda e: e.memset(ones5, 1.0), writes=['ones5'])
        S.add('act', lambda e: e.activation(out=e128, in_=f128, func=AF.Exp, scale=-1.0), reads=['f128'], writes=['e128'])
        S.add('act', lambda e: e.activation(out=f128, in_=e128, func=AF.Ln, bias=1.0, scale=1.0), reads=['e128'], writes=['sp_'])
        S.add('dve', lambda e: e.tensor_tensor_scan(out=cs, data0=ones5, data1=f128, initial=0.0, op0=ALU.mult, op1=ALU.add),
              reads=['sp_', 'ones5'], writes=['cs'])
        S.add('dve', lambda e: e.tensor_copy(out=tot, in_=cs[:, 511:512]), reads=['cs'], writes=['tot'])
        def s1c_part_b():
            S.enabled = stop_after >= 1
            bk, bkk = nb()
            S.add('pe', mm(bk[:, 0:1], tri, tot, True, True), reads=['tri', 'tot'], writes=[bkk])
            S.add('dve', lambda e, bk=bk: e.tensor_copy(out=offs, in_=bk[:, 0:1]), reads=[bkk], writes=['offs'])
            S.add('dve', lambda e: e.tensor_scalar(out=cs, in0=cs, scalar1=offs[:, 0:1], scalar2=None, op0=ALU.add), reads=['cs', 'offs'], writes=['cneg'])

            def split3(src, skey, rtmp, rkey, hi, mid, lo, pfx):
                S.add('dve', lambda e: e.tensor_copy(out=hi, in_=src), reads=[skey], writes=[pfx + 'hi'])
                S.add('dve', lambda e: e.tensor_tensor(out=rtmp, in0=src, in1=hi, op=ALU.subtract), reads=[skey, pfx + 'hi'], writes=[rkey])
                S.add('dve', lambda e: e.tensor_copy(out=mid, in_=rtmp), reads=[rkey], writes=[pfx + 'mid'])
                S.add('dve', lambda e: e.tensor_tensor(out=rtmp, in0=rtmp, in1=mid, op=ALU.subtract), reads=[rkey, pfx + 'mid'], writes=[rkey])
                S.add('dve', lambda e: e.tensor_copy(out=lo, in_=rtmp), reads=[rkey], writes=[pfx + 'lo'])

            split3(cs, 'cneg', rt, 'rt', khi, kmid, klo, 'k')
            qt3 = qtmp.rearrange("p (i t) -> p i t", i=2, t=128)
            for (ksrc, kkey, qdst, qkey) in [(khi, 'khi', qhi, 'qhi'), (kmid, 'kmid', qmid, 'qmid'), (klo, 'klo', qlo, 'qlo')]:
                k4 = ksrc.rearrange("p (i two t) -> p i two t", i=2, two=2, t=128)
                q3 = qdst.rearrange("p (i t) -> p i t", i=2, t=128)
                S.add('dve', lambda e, k4=k4: e.tensor_scalar(out=qt3, in0=k4[:, :, 0, :], scalar1=sel128[:, 0:1], scalar2=None, op0=ALU.mult),
                      reads=[kkey, 'sel128'], writes=['qtmp'])
                S.add('dve', lambda e, k4=k4, q3=q3: e.scalar_tensor_tensor(out=q3, in0=k4[:, :, 1, :], scalar=sel128[:, 1:2], in1=qt3, op0=ALU.mult, op1=ALU.add),
                      reads=[kkey, 'sel128', 'qtmp'], writes=[qkey])
            S.add('sp', [lambda e, r=r, src=src: e.dma_start(out=ck_s[r], in_=src) for r, src in enumerate([khi, kmid, klo])]
                  + [lambda e, r=r, src=src: e.dma_start(out=cq_s[r], in_=src) for r, src in enumerate([qhi, qmid, qlo])],
                  reads=['khi', 'kmid', 'klo', 'qhi', 'qmid', 'qlo'], writes=['ck_s', 'cq_s'], dkey='kc0')
            fns = []
            for r in range(3):
                fns.append(lambda e, r=r: e.dma_start(out=KT_s[:, 64 + r, :], in_=ck_s[r].rearrange("(h s) t -> h (s t)", s=16)))
                fns.append(lambda e, r=r: e.dma_start(out=QT_s[:, 67 + r, :], in_=cq_s[r].rearrange("(h s) t -> h (s t)", s=16)))
            S.add('sp', fns, reads=['ck_s', 'cq_s'],
                  writes=[('KTc', r) for r in range(64, 67)] + [('QTc', r) for r in range(67, 70)], dkey='kc')

            S.enabled = stop_after >= 2

        S.enabled = stop_after >= 2
        A = Arena(arena_t, PBASE, 25600)
        xs2 = [v3(A.f32(4096), 4, 1024) for _ in range(2)]
        xTb2 = [v3(A.bf16(4096), 8, 512) for _ in range(2)]
        qst = [v3(A.bf16(2048), 4, 512) for _ in range(2)]
        zt = A.f32(4 * 4 * 130).rearrange("p (c b t) -> p c b t", c=4, b=4, t=130)
        ctm = [A.f32(512) for _ in range(2)]
        ytm = [A.f32(512) for _ in range(2)]
        btm = [A.f32(512) for _ in range(2)]
        cst = [v3(A.bf16(2048), 4, 512) for _ in range(2)]
        zhalo = v3(A.f32(256), 4, 64)
        xhT = v3(A.bf16(512), 8, 64)
        xh = A.f32(1024, 64)
        chl = A.f32(64)

        wqk = [('Wq', k) for k in range(KC)]
        wbk = [('Wbch', k) for k in range(KC)]
        S.add('sp', lambda e: e.dma_start(out=xh, in_=xhalo), writes=['xh'], dkey='c5')
        bk, bkk = nb()
        S.add('pe', [lambda e, kc=kc, bk=bk: e.transpose(out=bk[:, kc * 64:(kc + 1) * 64], in_=xh[:, kc * 128:(kc + 1) * 128],
                                                         identity=identf[0:64, 0:64]) for kc in range(KC)], reads=['xh', 'identf'], writes=[bkk])
        S.add('act', lambda e, bk=bk: e.copy(out=xhT.rearrange("p k n -> p (k n)"), in_=bk[:, :]), reads=[bkk], writes=['xhT'])
        for cc in range(4):
            bc, bck = nb()
            bh, bhk = nb()
            S.add('pe', [mm(bc[:, 0:64], Wbch[:, kc, 512 + cc * 128:512 + (cc + 1) * 128], xhT[:, kc, :], kc == 0, kc == 7) for kc in range(KC)],
                  reads=['xhT'] + wbk, writes=[bck])
            S.add('pe', [mm(bh[:, 0:64], Wbch[:, kc, 1024 + cc * 128:1024 + (cc + 1) * 128], xhT[:, kc, :], kc == 0, kc == 7) for kc in range(KC)],
                  reads=['xhT'] + wbk, writes=[bhk])
            S.add('act', lambda e, bc=bc: e.copy(out=chl, in_=bc[:, 0:64]), reads=[bck], writes=['chl'])
            S.add('dve', lambda e, bh=bh, cc=cc: e.tensor_tensor(out=zhalo[:, cc, :], in0=bh[:, 0:64], in1=chl, op=ALU.mult),
                  reads=[bhk, 'chl'], writes=[('zhalo', cc)])

        def s2_load(t):
            sl = t % 2
            S.add('sp', lambda e, t=t, sl=sl: e.dma_start(out=xs2[sl], in_=xown[t * 512:(t + 1) * 512, :].rearrange("(b p) d -> p b d", p=128)),
                  writes=[('xs2', sl)], dkey=('xs2', sl))

        def s2_tr(t):
            sl = t % 2
            transposes_to_bf16(lambda b, sl=sl: xs2[sl][:, b, :], [('xs2', sl)], xTb2[sl], lambda kc, sl=sl: ('xTb2', sl, kc), 4)

        def s2_mm(t):
            sl = t % 2
            xkeys = [('xTb2', sl, kc) for kc in range(KC)]
            for hp in range(4):
                bk, bkk = nb()
                S.add('pe', [mm(bk[:, :], Wq[:, kc, hp * 128:(hp + 1) * 128], xTb2[sl][:, kc, :], kc == 0, kc == 7) for kc in range(KC)],
                      reads=xkeys + wqk, writes=[bkk])
                S.add('act', lambda e, bk=bk, hp=hp, sl=sl: e.activation(out=qst[sl][:, hp, :], in_=bk[:, :], func=AF.Copy, scale=0.125),
                      reads=[bkk], writes=[('qst', sl, hp)])
            S.add('sp', [lambda e, h=h, sl=sl, t=t: e.dma_start(out=QT_s[h, 0:64, t * 512:(t + 1) * 512],
                                                              in_=qst[sl][(h % 2) * 64:(h % 2 + 1) * 64, h // 2, :]) for h in range(H)],
                  reads=[('qst', sl, hp) for hp in range(4)], writes=[('QTd', h, t) for h in range(H)], dkey=('qst', sl))
            for cc in range(4):
                bb, bbk = nb()
                bc, bck = nb()
                bh, bhk = nb()
                for (bx, bxk, off) in [(bb, bbk, 0), (bc, bck, 512), (bh, bhk, 1024)]:
                    S.add('pe', [mm(bx[:, :], Wbch[:, kc, off + cc * 128:off + (cc + 1) * 128], xTb2[sl][:, kc, :], kc == 0, kc == 7) for kc in range(KC)],
                          reads=xkeys + wbk, writes=[bxk])
                ci = cc % 2
                S.add('act', lambda e, bc=bc, ci=ci: e.copy(out=ctm[ci], in_=bc[:, :]), reads=[bck], writes=[('ctm', ci)])
                S.add('act', lambda e, bb=bb, ci=ci: e.copy(out=btm[ci], in_=bb[:, :]), reads=[bbk], writes=[('btm', ci)])
                S.add('pool', lambda e, cc=cc, t=t: e.tensor_copy(out=zt[:, cc, :, 0:2], in_=zhalo[:, cc, 8 * t:8 * t + 8].rearrange("p (b j) -> p b j", j=2)),
                      reads=[('zhalo', cc)], writes=[('zth', cc)])
                S.add('dve', lambda e, bh=bh, cc=cc, ci=ci: e.tensor_tensor(out=zt[:, cc, :, 2:130], in0=bh[:, :].rearrange("p (b t) -> p b t", b=4),
                                                                     in1=ctm[ci].rearrange("p (b t) -> p b t", b=4), op=ALU.mult),
                      reads=[bhk, ('ctm', ci)], writes=[('zt', cc)])
                y3 = ytm[ci].rearrange("p (b t) -> p b t", b=4)
                S.add('dve', lambda e, cc=cc, y3=y3: e.tensor_scalar(out=y3, in0=zt[:, cc, :, 0:128], scalar1=cw[:, cc * 3:cc * 3 + 1], scalar2=None, op0=ALU.mult),
                      reads=[('zt', cc), ('zth', cc), 'cw'], writes=[('ytm', ci)])
                for kk in (1, 2):
                    S.add('dve', lambda e, cc=cc, y3=y3, kk=kk: e.scalar_tensor_tensor(out=y3, in0=zt[:, cc, :, kk:kk + 128], scalar=cw[:, cc * 3 + kk:cc * 3 + kk + 1],
                                                                                in1=y3, op0=ALU.mult, op1=ALU.add),
                          reads=[('zt', cc), ('zth', cc), 'cw', ('ytm', ci)], writes=[('ytm', ci)])
                S.add('dve', lambda e, cc=cc, ci=ci, sl=sl: e.tensor_tensor(out=cst[sl][:, cc, :], in0=btm[ci], in1=ytm[ci], op=ALU.mult),
                      reads=[('btm', ci), ('ytm', ci)], writes=[('cst', sl, cc)])
            S.add('sp', lambda e, sl=sl, t=t: e.dma_start(out=mixT_s[512:1024, t * 512:(t + 1) * 512].rearrange("(c p) t -> p c t", p=128), in_=cst[sl]),
                  reads=[('cst', sl, cc) for cc in range(4)], writes=[('mixc', t)], dkey=('cst', sl))

        s2_load(0)
        s2_load(1)
        s2_tr(0)
        for t in range(8):
            if t + 1 < 8:
                s2_tr(t + 1)
            if t + 2 < 8:
                s2_load(t + 2)
            s2_mm(t)
            if t == 1:
                s1c_part_b()

        S.enabled = stop_after >= 3
        if True:
            H0 = Arena(arena_t, 25600, W2TOP)
            KTh0 = H0.bf16(SEQ)
            Vh0 = v3(H0.bf16(64 * 128), 64, 128)
            QTh0 = H0.bf16(NOWN)
            S.add('sp', lambda e: e.dma_start(out=KTh0[0:70, :], in_=KT_s[0]),
                  reads=[('KTd', 0, t) for t in range(NT1)] + [('KTc', r) for r in range(64, 70)], writes=[('KTh', 0)], dkey=('hdK', 0))
            S.add('sp', lambda e: e.dma_start(out=Vh0, in_=V_s[0]), reads=[('Vd', t) for t in range(NT1)], writes=[('Vh', 0)], dkey=('hdV', 0))
            S.add('sp', lambda e: e.dma_start(out=QTh0[0:70, :], in_=QT_s[0]),
                  reads=[('QTd', 0, t) for t in range(8)] + [('QTc', r) for r in range(64, 70)], writes=[('QTh', 0)], dkey=('hdQ', 0))
            S.barrier()
            pre4a = prefetch_ln_weights(8, w_out0, mix_g[0:1, :], mix_b[0:1, :]) if S.enabled else None
            A = Arena(arena_t, PBASE, 25600)
            KTh = [KTh0, A.bf16(SEQ)]
            Vh = [Vh0, v3(A.bf16(64 * 128), 64, 128)]
            QTh = [QTh0, A.bf16(NOWN)]
            pT = [v3(A.bf16(1024), 2, 512) for _ in range(3)]
            rl = [A.f32(512) for _ in range(2)]
            ost = [A.bf16(NOWN) for _ in range(2)]
            ps_s = [(bpairs[i], [('ps', 2 * i), ('ps', 2 * i + 1)]) for i in range(3)]
            ps_o = [(banks[6 + i], ('ps', 6 + i)) for i in range(2)]
            steps = []
            for h in range(H):
                for qt in range(8):
                    i0 = 4 * qt
                    nch = 2 * i0 + 8
                    for j in range(0, nch, 2):
                        steps.append((h, qt, j, nch))

            def load_head(h):
                hs = h % 2
                S.add('sp', lambda e: e.dma_start(out=KTh[hs][0:70, :], in_=KT_s[h]),
                      reads=[('KTd', h, t) for t in range(NT1)] + [('KTc', r) for r in range(64, 70)], writes=[('KTh', hs)], dkey=('hdK', hs))
                S.add('sp', lambda e: e.dma_start(out=Vh[hs], in_=V_s[h]), reads=[('Vd', t) for t in range(NT1)], writes=[('Vh', hs)], dkey=('hdV', hs))
                S.add('sp', lambda e: e.dma_start(out=QTh[hs][0:70, :], in_=QT_s[h]),
                      reads=[('QTd', h, t) for t in range(8)] + [('QTc', r) for r in range(64, 70)], writes=[('QTh', hs)], dkey=('hdQ', hs))

            LA = 2
            n = len(steps)
            for idx in range(n + LA):
                if idx < n:
                    h, qt, j, nch = steps[idx]
                    hs = h % 2
                    i0 = 4 * qt
                    lb = max(i0, j // 2) - i0
                    c0 = lb * 128
                    diag = (j // 2) >= i0
                    bp, bpk = ps_s[idx % 3]
                    fns = []
                    for u in range(2):
                        fns.append(mm(bp[:, u * 512 + c0:(u + 1) * 512], KTh[hs][0:70, (j + u) * 128:(j + u + 1) * 128],
                                      QTh[hs][0:70, qt * 512 + c0:(qt + 1) * 512], True, not diag))
                        if diag:
                            fns.append(mm(bp[:, u * 512 + c0:u * 512 + c0 + 128], identb, maskb[:, u * 128:(u + 1) * 128], False, True))
                    rds = [('KTh', hs), ('QTh', hs)] + (['identb', 'maskb'] if diag else [])
                    S.add('pe', fns, reads=rds, writes=bpk)
                    S.add('act', lambda e, bp=bp, c0=c0, idx=idx: e.activation(out=pT[idx % 3][:, :, c0:512],
                                                                             in_=bp[:, :].rearrange("p (u t) -> p u t", u=2)[:, :, c0:512], func=AF.Exp),
                          reads=bpk, writes=[('pT', idx % 3)])
                k = idx - LA
                if k >= 0:
                    h, qt, j, nch = steps[k]
                    hs = h % 2
                    i0 = 4 * qt
                    lb = max(i0, j // 2) - i0
                    c0 = lb * 128
                    tix = h * 8 + qt
                    bo, bok = ps_o[tix % 2]
                    if qt == 0 and j == 0 and h + 1 < H:
                        load_head(h + 1)
                        drain_todo(pre4a, 2)
                    S.add('pe', [mm(bo[:, c0:512], Vh[hs][:, j + u, :], pT[k % 3][:, u, c0:512], (j + u) == 0, (j + u) == nch - 1) for u in range(2)],
                          reads=[('Vh', hs), ('pT', k % 3)], writes=[bok])
                    if j + 2 == nch:
                        r = rl[tix % 2]
                        S.add('dve', lambda e, bo=bo, r=r: e.reciprocal(out=r[64:128, :], in_=bo[64:128, :]), reads=[bok], writes=[('rl', tix % 2)])
                        S.add('dve', lambda e, bo=bo, r=r, hs=hs, qt=qt: e.tensor_tensor(out=ost[hs][0:64, qt * 512:(qt + 1) * 512], in0=bo[0:64, :],
                                                                                     in1=r[64:128, :], op=ALU.mult),
                              reads=[bok, ('rl', tix % 2)], writes=[('ost', hs)])
                        if qt == 7:
                            S.add('sp', lambda e, h=h, hs=hs: e.dma_start(out=mixT_s[h * 64:(h + 1) * 64, :], in_=ost[hs][0:64, :]),
                                  reads=[('ost', hs)], writes=[('mixa', h)], dkey=('ost', hs))

        def stage_ln(inT_s, in_keys, nch, w_d, res_s, res_keys, g_row, b_row, xo_s, xo_key, xT_o, xTo_key, TT, LAG, pre=None, gset=0, mid_pre=None, reserve_hi=None):
            S.barrier()
            NB = TT // 128
            NTL = NOWN // TT
            mp = mid_pre() if (mid_pre is not None and S.enabled) else None
            if pre is not None and S.enabled:
                Wt, g_t, b_t, top, _todo = pre
                drain_todo(pre, 999)
                A = Arena(arena_t, PBASE, min(top, mp[3] if mp is not None else top, reserve_hi if reserve_hi is not None else top))
            else:
                A = new_arena()
                Wt = v3(A.bf16(nch * 1024), nch, 1024)
                g_t = A.f32(1024)
                b_t = A.f32(1024)
            mT = [v3(A.bf16(nch * TT), nch, TT) for _ in range(2)]
            xr = [v3(A.f32(NB * 1024), NB, 1024) for _ in range(2)]
            NZ = LAG + 2
            zs = [A.f32(1024) for _ in range(NZ)]
            xo = [v3(A.f32(NB * 1024), NB, 1024) for _ in range(2)]
            xTo = [v3(A.bf16(8 * TT), 8, TT) for _ in range(2)]
            if pre is None or not S.enabled:
                load_w(Wt, w_d, 'Wt', nch, 'w')
                load_bcast(g_t, g_row, 'g_t', 'c6')
                load_bcast(b_t, b_row, 'b_t', 'c7')
            wk = [('Wt', k) for k in range(nch)]

            def loads(t):
                sl = t % 2
                S.add('sp', lambda e: e.dma_start(out=mT[sl], in_=inT_s[:, t * TT:(t + 1) * TT].rearrange("(k p) t -> p k t", p=128)),
                      reads=in_keys(t), writes=[('mT', sl)], dkey=('mT', sl))
                S.add('sp', lambda e: e.dma_start(out=xr[sl], in_=res_s[t * TT:(t + 1) * TT, :].rearrange("(b p) d -> p b d", p=128)),
                      reads=res_keys(t), writes=[('xr', sl)], dkey=('xr', sl))

            def emit_tr(t, b, zi):
                sl = t % 2
                for hf in range(2):
                    bk, bkk = nb()
                    S.add('pe', [lambda e, q=q, bk=bk, hf=hf: e.transpose(out=bk[:, q * 128:(q + 1) * 128],
                                                                         in_=zs[zi][:, (hf * 4 + q) * 128:(hf * 4 + q + 1) * 128],
                                                                         identity=identf) for q in range(4)],
                          reads=[('zs', zi), 'identf'], writes=[bkk])
                    for q in range(4):
                        kc = hf * 4 + q
                        S.add('act', lambda e, bk=bk, q=q, kc=kc: e.activation(out=xTo[sl][:, kc, b * 128:(b + 1) * 128], in_=bk[:, q * 128:(q + 1) * 128],
                                                                             func=AF.Identity, bias=lnbc[:, gset * 8 + kc:gset * 8 + kc + 1],
                                                                             scale=lngc[:, gset * 8 + kc:gset * 8 + kc + 1]),
                              reads=[bkk, 'lngc', 'lnbc'], writes=[('xTo', sl, b, kc)])
                if b == NB - 1:
                    S.add('sp', lambda e: e.dma_start(out=xT_o[:, t * TT:(t + 1) * TT].rearrange("(k p) t -> p k t", p=128), in_=xTo[sl]),
                          reads=[('xTo', sl, bb, kc) for bb in range(NB) for kc in range(8)], writes=[(xTo_key, t)], dkey=('xTo', sl))

            pending = []
            deferred_gb = None
            loads(0)
            g = 0
            for t in range(NTL):
                sl = t % 2
                if t + 1 < NTL:
                    loads(t + 1)
                drain_todo(mp, 1)
                for b in range(NB):
                    zi = g % NZ
                    g += 1
                    for half in range(2):
                        bk, bkk = nb()
                        S.add('pe', [mm(bk[:, :], mT[sl][:, c, b * 128:(b + 1) * 128], Wt[:, c, half * 512:(half + 1) * 512], c == 0, c == nch - 1) for c in range(nch)],
                              reads=[('mT', sl)] + wk, writes=[bkk])
                        S.add('dve', lambda e, bk=bk, b=b, half=half, sl=sl, zi=zi: e.scalar_tensor_tensor(
                            out=zs[zi][:, half * 512:(half + 1) * 512], in0=xr[sl][:, b, half * 512:(half + 1) * 512], scalar=ALPHA, in1=bk[:, :],
                            op0=ALU.mult, op1=ALU.add), reads=[bkk, ('xr', sl)], writes=[('zs', zi)])
                    use_dve = (nch == 8) and (g % 2 == 0) and (b != NB - 1)
                    new_gb = layer_norm(zs[zi], ('zs', zi), xo[sl][:, b, :], ('xo', sl, b), g_t, b_t, 'g_t', 'b_t', act_norm=True,
                                        gb_eng=('dve' if use_dve else 'pool'), defer_gb=use_dve)
                    if deferred_gb is not None:
                        deferred_gb()
                    deferred_gb = new_gb
                    if b == NB - 1:
                        S.add('sp', lambda e, t=t, sl=sl: e.dma_start(out=xo_s[t * TT:(t + 1) * TT, :].rearrange("(b p) d -> p b d", p=128), in_=xo[sl]),
                              reads=[('xo', sl, bb) for bb in range(NB)], writes=[(xo_key, t)], dkey=('xo', sl))
                    if xT_o is not None:
                        pending.append((t, b, zi))
                        if len(pending) > LAG:
                            emit_tr(*pending.pop(0))
            while pending:
                emit_tr(*pending.pop(0))
            drain_todo(mp, 999)
            return mp

        def stage_ffn_in(xT_i, xTi_key, nkeys, w1_d, aT_key, next_pre=None, extra_pre=None):
            S.barrier()
            nxt = next_pre() if (next_pre is not None and S.enabled) else None
            xp = extra_pre() if (extra_pre is not None and S.enabled) else None
            lim = nxt[3] if nxt is not None else AW
            if xp is not None:
                lim = min(lim, xp[3])
            A = Arena(arena_t, PBASE, lim)
            xTa = v3(A.bf16(8 * NOWN), 8, NOWN)
            W1c = [v3(A.bf16(8 * 256), 8, 256) for _ in range(3)]
            sg = [A.bf16(512) for _ in range(2)]
            ast = [A.bf16(1024) for _ in range(4)]
            for t in range(8):
                S.add('sp', lambda e, t=t: e.dma_start(out=xTa[:, :, t * 512:(t + 1) * 512], in_=xT_i[:, t * 512:(t + 1) * 512].rearrange("(k p) t -> p k t", p=128)),
                      reads=[(xTi_key, t), (xTi_key, 2 * t), (xTi_key, 2 * t + 1)], writes=[('xTa', t)], dkey=('xTa', t))
            ctr = 0
            def w1_load(j):
                ws = j % 3
                S.add('pool', lambda e: e.dma_start(out=W1c[ws][:, :, 0:128], in_=w1_d[:, j * 128:(j + 1) * 128].rearrange("(k p) n -> p k n", p=128)),
                      writes=[('W1g', ws)], dkey=('W1g', ws))
                S.add('pool', lambda e: e.dma_start(out=W1c[ws][:, :, 128:256], in_=w1_d[:, FH + j * 128:FH + (j + 1) * 128].rearrange("(k p) n -> p k n", p=128)),
                      writes=[('W1u', ws)], dkey=('W1u', ws))

            w1_load(0)
            w1_load(1)
            for j in range(NJ):
                ws = j % 3
                if j + 2 < NJ:
                    w1_load(j + 2)
                drain_todo(nxt, 2)
                if j >= 11:
                    drain_todo(xp, 1)
                for t in range(8):
                    bg, bgk = nb()
                    bu, buk = nb()
                    S.add('pe', [mm(bg[:, :], W1c[ws][:, kc, 0:128], xTa[:, kc, t * 512:(t + 1) * 512], kc == 0, kc == 7) for kc in range(KC)],
                          reads=[('xTa', t), ('W1g', ws)], writes=[bgk])
                    S.add('pe', [mm(bu[:, :], W1c[ws][:, kc, 128:256], xTa[:, kc, t * 512:(t + 1) * 512], kc == 0, kc == 7) for kc in range(KC)],
                          reads=[('xTa', t), ('W1u', ws)], writes=[buk])
                    si = ctr % 2
                    ai = (ctr // 2) % 4
                    hf = ctr % 2
                    ctr += 1
                    S.add('act', lambda e, bg=bg, si=si: e.activation(out=sg[si], in_=bg[:, :], func=AF.Silu), reads=[bgk], writes=[('sg', si)])
                    S.add('dve', lambda e, bu=bu, si=si, ai=ai, hf=hf: e.tensor_tensor(out=ast[ai][:, hf * 512:(hf + 1) * 512], in0=bu[:, :], in1=sg[si], op=ALU.mult),
                          reads=[buk, ('sg', si)], writes=[('ast', ai, hf)])
                    if hf == 1:
                        S.add('sp', lambda e, j=j, t=t, ai=ai: e.dma_start(out=aT_s[j * 128:(j + 1) * 128, (t - 1) * 512:(t + 1) * 512], in_=ast[ai]),
                              reads=[('ast', ai, 0), ('ast', ai, 1)], writes=[(aT_key, j, t - 1), (aT_key, j, t)], dkey=('ast', ai))

            drain_todo(xp, 999)
            return (nxt, xp)

        def stage_gmlp(xT_i, xTi_key, next_pre=None, pre_w=None):
            S.barrier()
            nxt = next_pre() if (next_pre is not None and S.enabled) else None
            hi_lim = nxt[3] if nxt is not None else AW
            if pre_w is not None and S.enabled:
                Wuv = pre_w[0]
                drain_todo(pre_w, 999)
                A = Arena(arena_t, PBASE, GM_LO)
                A2 = Arena(arena_t, GM_HI, hi_lim)
                g_t = A.f32(1024)
                b_t = A.f32(1024)
                load_bcast(g_t, vln_g[0:1, :], 'vg_t', 'c10')
                load_bcast(b_t, vln_b[0:1, :], 'vb_t', 'c11')
            else:
                A = Arena(arena_t, PBASE, hi_lim)
                A2 = A
                Wuv = v3(A.bf16(8 * 2048), 8, 2048)
                g_t = A.f32(1024)
                b_t = A.f32(1024)
            wsT = v3(A.bf16(1024), 8, 128)
            wsTf = A.f32(1024)
            bsf = A.f32(1024, 1)
            bsr = A.f32(1024, 1)
            bshi = A.bf16(1024, 1)
            bslo = A.bf16(1024, 1)
            xTi = [v3(A.bf16(4096), 8, 512) for _ in range(2)]
            uT = [v3(A.f32(4096), 8, 512) for _ in range(2)]
            vln = [A.bf16(1024) for _ in range(3)]
            gst = [v3(A2.bf16(4096), 8, 512) for _ in range(2)]
            vs = [A2.f32(1024) for _ in range(3)]
            if pre_w is None or not S.enabled:
                load_w(Wuv, w_in1, 'Wuv', 8, 'w')
                load_bcast(g_t, vln_g[0:1, :], 'vg_t', 'c10')
                load_bcast(b_t, vln_b[0:1, :], 'vb_t', 'c11')
            S.add('sp', lambda e: e.dma_start(out=wsTf, in_=wsT_d), writes=['wsTf'], dkey='c8')
            S.add('dve', lambda e: e.tensor_copy(out=wsT.rearrange("p g i -> p (g i)"), in_=wsTf), reads=['wsTf'], writes=['wsT'])
            S.add('dve', lambda e: e.memset(wsT[64:128, :, 0:64], 0.0), reads=['wsT'], writes=['wsT'])
            S.add('sp', lambda e: e.dma_start(out=bsf, in_=b_s), writes=['bsf'], dkey='c9')
            S.add('dve', lambda e: e.tensor_copy(out=bshi, in_=bsf), reads=['bsf'], writes=['bshi'])
            S.add('dve', lambda e: e.tensor_tensor(out=bsr, in0=bsf, in1=bshi, op=ALU.subtract), reads=['bsf', 'bshi'], writes=['bsr'])
            S.add('dve', lambda e: e.tensor_copy(out=bslo, in_=bsr), reads=['bsr'], writes=['bslo'])
            wk = [('Wuv', k) for k in range(KC)]
            def g_load(t):
                sl = t % 2
                S.add('sp', lambda e: e.dma_start(out=xTi[sl], in_=xT_i[:, t * 512:(t + 1) * 512].rearrange("(k p) t -> p k t", p=128)),
                      reads=[(xTi_key, 2 * t), (xTi_key, 2 * t + 1)], writes=[('xTi', sl)], dkey=('xTi', sl))

            def emit_spatial(t, b, vi):
                sl = t % 2
                for hf in range(2):
                    bk, bkk = nb()
                    fns = []
                    for q in range(4):
                        g = hf * 4 + q
                        o_ = bk[:, q * 128:(q + 1) * 128]
                        fns.append(mm(o_, vln[vi][:, g * 128:(g + 1) * 128], wsT[:, g, :], True, False))
                        fns.append(mm(o_, onesb[0:1, 0:128], bshi[0:1, g * 128:(g + 1) * 128], False, False))
                        fns.append(mm(o_, onesb[0:1, 0:128], bslo[0:1, g * 128:(g + 1) * 128], False, True))
                    S.add('pe', fns, reads=[('vln', vi), 'wsT', 'onesb', 'bshi', 'bslo'], writes=[bkk])
                    S.add('dve', lambda e, bk=bk, hf=hf: e.tensor_tensor(out=gst[sl][:, hf * 4:(hf + 1) * 4, b * 128:(b + 1) * 128],
                                                                       in0=bk[:, :].rearrange("p (q t) -> p q t", q=4),
                                                                       in1=uT[sl][:, hf * 4:(hf + 1) * 4, b * 128:(b + 1) * 128], op=ALU.mult),
                          reads=[bkk] + [('uT', sl, hf * 4 + q) for q in range(4)], writes=[('gst', sl, b, hf)])
                if b == 3:
                    S.add('sp', lambda e: e.dma_start(out=gT_s[:, t * 512:(t + 1) * 512].rearrange("(k p) t -> p k t", p=128), in_=gst[sl]),
                          reads=[('gst', sl, bb, hf) for bb in range(4) for hf in range(2)], writes=[('gT', t)], dkey=('gst', sl))

            g_load(0)
            pending = []
            gctr = 0
            LAGG = 2
            for t in range(8):
                sl = t % 2
                if t + 1 < 8:
                    g_load(t + 1)
                drain_todo(nxt, 2)
                for c in range(8):
                    bk, bkk = nb()
                    S.add('pe', [mm(bk[:, :], Wuv[:, kc, c * 128:(c + 1) * 128], xTi[sl][:, kc, :], kc == 0, kc == 7) for kc in range(KC)],
                          reads=[('xTi', sl)] + wk, writes=[bkk])
                    S.add('act', lambda e, bk=bk, c=c, sl=sl: e.activation(out=uT[sl][:, c, :], in_=bk[:, :], func=AF.Gelu), reads=[bkk], writes=[('uT', sl, c)])
                for b in range(4):
                    vi = gctr % 3
                    gctr += 1
                    for half in range(2):
                        bk, bkk = nb()
                        S.add('pe', [mm(bk[:, :], xTi[sl][:, kc, b * 128:(b + 1) * 128], Wuv[:, kc, 1024 + half * 512:1024 + (half + 1) * 512], kc == 0, kc == 7) for kc in range(KC)],
                              reads=[('xTi', sl)] + wk, writes=[bkk])
                        S.add('act', lambda e, bk=bk, half=half, vi=vi: e.activation(out=vs[vi][:, half * 512:(half + 1) * 512], in_=bk[:, :], func=AF.Gelu),
                              reads=[bkk], writes=[('vs', vi)])
                    layer_norm(vs[vi], ('vs', vi), vln[vi], ('vln', vi), g_t, b_t, 'vg_t', 'vb_t')
                    pending.append((t, b, vi))
                    if len(pending) > LAGG:
                        emit_spatial(*pending.pop(0))
            while pending:
                emit_spatial(*pending.pop(0))
            return nxt

        S.enabled = stop_after >= 4
        stage_ln(mixT_s, lambda t: [('mixc', t)] + [('mixa', h) for h in range(H)], 8, w_out0, xown, lambda t: [],
                 mix_g[0:1, :], mix_b[0:1, :], x1_s, 'x1', xT_s, 'x1T', 512, 3, pre=pre4a, gset=0)
        S.enabled = stop_after >= 5
        pre5a, pregm = stage_ffn_in(xT_s, 'x1T', 8, ffn_w1[0], 'aT0',
                                    next_pre=lambda: prefetch_ln_weights(NJ, ffn_w2[0], ffn_g[0:1, :], ffn_b[0:1, :]),
                                    extra_pre=prefetch_gmlp_weights)
        S.enabled = stop_after >= 6
        stage_ln(aT_s, lambda t: [('aT0', j, t // 2) for j in range(NJ)], NJ, ffn_w2[0], x1_s, lambda t: [('x1', t // 2)],
                 ffn_g[0:1, :], ffn_b[0:1, :], x2_s, 'x2', xT_s, 'x2T', 256, 3, pre=pre5a, gset=1, reserve_hi=GM_LO)
        S.enabled = stop_after >= 7
        pre6a = stage_gmlp(xT_s, 'x2T', next_pre=lambda: prefetch_ln_weights(8, w_out1, mix_g[1:2, :], mix_b[1:2, :]), pre_w=pregm)
        S.enabled = stop_after >= 8
        stage_ln(gT_s, lambda t: [('gT', t)], 8, w_out1, x2_s, lambda t: [('x2', 2 * t), ('x2', 2 * t + 1)],
                 mix_g[1:2, :], mix_b[1:2, :], x3_s, 'x3', xT_s, 'x3T', 512, 3, pre=pre6a, gset=2)
        S.enabled = stop_after >= 9
        pre7, _ = stage_ffn_in(xT_s, 'x3T', 8, ffn_w1[1], 'aT1',
                               next_pre=lambda: prefetch_ln_weights(NJ, ffn_w2[1], ffn_g[1:2, :], ffn_b[1:2, :]))
        S.enabled = stop_after >= 10
        stage_ln(aT_s, lambda t: [('aT1', j, t // 2) for j in range(NJ)], NJ, ffn_w2[1], x3_s, lambda t: [('x3', t // 2)],
                 ffn_g[1:2, :], ffn_b[1:2, :], out, 'out', None, None, 256, 2, pre=pre7, gset=3)
        S.enabled = True
        S.emit(st)
        nc._n_sems = S.n_sems
        nc._n_ops = {e: len(S.ops[e]) for e in ENGS}
    return nc


def make_in_maps(inputs):
    f = lambda a: np.ascontiguousarray(np.asarray(a, dtype=np.float32))
    x = f(inputs["x"])
    shared = {
        "w_in0": f(inputs["even_w_in"][0]),
        "b_f": f(inputs["even_b_f"][0]).reshape(8, 1),
        "cwl": f(np.asarray(inputs["even_conv_w"][0]).reshape(3, 4, 128).transpose(2, 1, 0).reshape(128, 12)),
        "w_out0": f(inputs["even_w_out"][0]),
        "w_in1": f(inputs["odd_w_in"][0]),
        "vln_g": f(inputs["odd_v_ln_g"][0]).reshape(1, D),
        "vln_b": f(inputs["odd_v_ln_b"][0]).reshape(1, D),
        "wsT": f(np.asarray(inputs["odd_w_s"][0]).transpose(2, 0, 1).reshape(128, 1024)),
        "b_s": f(inputs["odd_b_s"][0]).reshape(1, D),
        "w_out1": f(inputs["odd_w_out"][0]),
        "mix_g": f(inputs["mix_ln_g"]), "mix_b": f(inputs["mix_ln_b"]),
        "ffn_w1": f(inputs["ffn_w_in"]), "ffn_w2": f(inputs["ffn_w_out"]),
        "ffn_g": f(inputs["ffn_ln_g"]), "ffn_b": f(inputs["ffn_ln_b"]),
    }
    gsets = [inputs["mix_ln_g"][0], inputs["ffn_ln_g"][0], inputs["mix_ln_g"][1], inputs["ffn_ln_g"][1]]
    bsets = [inputs["mix_ln_b"][0], inputs["ffn_ln_b"][0], inputs["mix_ln_b"][1], inputs["ffn_ln_b"][1]]
    shared["lngc"] = f(np.stack([np.asarray(g).reshape(8, 128).T for g in gsets], axis=1).reshape(128, 32))
    shared["lnbc"] = f(np.stack([np.asarray(b).reshape(8, 128).T for b in bsets], axis=1).reshape(128, 32))
    pp = np.arange(128)
    shared["triseg"] = ((pp[:, None] // 16 == pp[None, :] // 16) & (pp[:, None] % 16 < pp[None, :] % 16)).astype(np.float32)
    s_idx = np.arange(128)[:, None]
    t_idx = np.arange(128)[None, :]
    tri = np.where(s_idx <= t_idx, 0.0, NEG).astype(np.float32)
    full0 = np.zeros((128, 128), np.float32)
    fulln = np.full((128, 128), NEG, np.float32)
    in_maps = []
    for c in range(8):
        b, par = c // 2, c % 2
        xb = x[b]
        blocks = xb.reshape(64, 128, D)
        own = np.ascontiguousarray(blocks[par::2]).reshape(NOWN, D)
        halo = np.zeros((32, 2, D), np.float32)
        for i in range(32):
            g = 2 * i + par
            if g > 0:
                halo[i] = xb[g * 128 - 2:g * 128]
        mask = np.concatenate([tri, fulln], axis=1) if par == 0 else np.concatenate([full0, tri], axis=1)
        sel = np.tile(np.array([[-(1.0 - par), -float(par)]], np.float32), (8, 1))
        m = dict(shared)
        m.update({"xfull": xb, "xown": own, "xhalo": halo.reshape(64, D), "maskd": np.ascontiguousarray(mask), "selv": sel})
        in_maps.append(m)
    return in_maps


_NC = None


def kernel(**inputs):
    global _NC
    if _NC is None:
        _NC = build()
    in_maps = make_in_maps(inputs)
    res = run_bass_kernel_spmd(_NC, in_maps, core_ids=list(range(8)))
    outf = np.empty((4, 64, 128, D), np.float32)
    for c in range(8):
        b, par = c // 2, c % 2
        outf[b, par::2] = np.asarray(res.results[c]["out"]).reshape(32, 128, D)
    return outf.reshape(4, SEQ, D)
```

```python
import numpy as np
from contextlib import ExitStack
import concourse.bass as bass
import concourse.mybir as mybir
from concourse.bass_utils import run_bass_kernel_spmd

F32 = mybir.dt.float32
BF16 = mybir.dt.bfloat16
AF = mybir.ActivationFunctionType
ALU = mybir.AluOpType

D = 1024
KC = 8
SEQ = 8192
NOWN = 4096
H = 8
FH = 2816
NJ = 22
ALPHA = float((2.0 * 2) ** 0.25)
EPS = 1e-5
NEG = -30000.0
ENGS = ['pe', 'act', 'dve', 'pool', 'sp']
EPOCH = 20000


class Op:
    __slots__ = ('eng', 'fns', 'is_dma', 'dkey', 'dval', 'idx', 'deps', 'needed', 'sig')


class Sched:
    def __init__(self, nc):
        self.nc = nc
        self.ops = {e: [] for e in ENGS}
        self.last_writer = {}
        self.readers = {}
        self.dma_cum = {}
        self.bar_deps = []
        self.pending_dma = []
        self.enabled = True

    def barrier(self):
        deps = []
        for e in ENGS:
            for op in reversed(self.ops[e]):
                if not op.is_dma:
                    deps.append(op)
                    break
        last = {}
        for op in self.pending_dma:
            last[op.dkey] = op
        deps.extend(last.values())
        self.pending_dma = list(last.values())
        self.bar_deps = deps

    def add(self, eng, fns, reads=(), writes=(), dkey=None):
        if not self.enabled:
            return None
        if not isinstance(fns, (list, tuple)):
            fns = [fns]
        op = Op()
        op.eng = eng
        op.fns = list(fns)
        op.is_dma = dkey is not None
        op.dkey = dkey
        op.needed = False
        op.sig = None
        op.dval = 0
        if op.is_dma:
            self.dma_cum[dkey] = self.dma_cum.get(dkey, 0) + 16 * len(op.fns)
            op.dval = self.dma_cum[dkey]
            self.pending_dma.append(op)
        deps = {id(o): o for o in self.bar_deps}
        for r in reads:
            w = self.last_writer.get(r)
            if w is not None:
                deps[id(w)] = w
        for k in writes:
            w = self.last_writer.get(k)
            if w is not None:
                deps[id(w)] = w
            rd = self.readers.get(k)
            if rd:
                for o in rd['eng'].values():
                    deps[id(o)] = o
                for o in rd['dma']:
                    deps[id(o)] = o
        deps.pop(id(op), None)
        op.deps = list(deps.values())
        for r in reads:
            rd = self.readers.setdefault(r, {'eng': {}, 'dma': []})
            if op.is_dma:
                rd['dma'].append(op)
            else:
                rd['eng'][eng] = op
        for k in writes:
            self.last_writer[k] = op
            self.readers[k] = {'eng': {}, 'dma': []}
        op.idx = len(self.ops[eng])
        self.ops[eng].append(op)
        return op

    def emit(self, stack):
        nc = self.nc
        for e in ENGS:
            for op in self.ops[e]:
                for d in op.deps:
                    if d.is_dma:
                        continue
                    if d.eng == op.eng and d.eng == 'pe':
                        continue
                    d.needed = True
        eng_sems = {}
        for e in ENGS:
            cnt = 0
            for op in self.ops[e]:
                if op.is_dma:
                    continue
                if op.needed:
                    op.sig = ((e, cnt // EPOCH), cnt % EPOCH + 1)
                    cnt += 1
            for ep in range((cnt + EPOCH - 1) // EPOCH):
                eng_sems[(e, ep)] = stack.enter_context(nc.semaphore(f"s_{e}_{ep}"))
        dma_sems = {}
        for k in self.dma_cum:
            dma_sems[k] = stack.enter_context(nc.semaphore(f"d_{len(dma_sems)}"))
        self.n_sems = len(eng_sems) + len(dma_sems)
        block = stack.enter_context(nc.Block())

        def run_engine(e, eng):
            waited = {}
            for op in self.ops[e]:
                w = {}
                for d in op.deps:
                    if d.is_dma:
                        key, val = ('d', d.dkey), d.dval
                    else:
                        if d.eng == e and e == 'pe':
                            continue
                        key, val = d.sig
                    if w.get(key, 0) < val:
                        w[key] = val
                for key, val in w.items():
                    if waited.get(key, 0) >= val:
                        continue
                    waited[key] = val
                    sem = dma_sems[key[1]] if key[0] == 'd' else eng_sems[key]
                    eng.wait_ge(sem, val)
                ins = None
                for f in op.fns:
                    ins = f(eng)
                    if op.is_dma:
                        ins.then_inc(dma_sems[op.dkey], 16)
                if (not op.is_dma) and op.sig is not None:
                    ins.then_inc(eng_sems[op.sig[0]], 1)
            if e == 'sp':
                for k, v in self.dma_cum.items():
                    if waited.get(('d', k), 0) < v:
                        eng.wait_ge(dma_sems[k], v)

        @block.tensor
        def _(eng):
            run_engine('pe', eng)

        @block.scalar
        def _(eng):
            run_engine('act', eng)

        @block.vector
        def _(eng):
            run_engine('dve', eng)

        @block.gpsimd
        def _(eng):
            run_engine('pool', eng)

        @block.sync
        def _(eng):
            run_engine('sp', eng)


class Arena:
    def __init__(self, ap, base, limit):
        self.ap = ap
        self.off = base
        self.limit = limit

    def f32(self, n, parts=128):
        n8 = (n + 7) // 8 * 8
        a = self.ap[0:parts, self.off:self.off + n]
        self.off += n8
        assert self.off <= self.limit, (self.off, self.limit)
        return a

    def bf16(self, n, parts=128):
        w = (n + 1) // 2
        return self.f32(w, parts).bitcast(BF16)[:, 0:n]


def mm(out, lhsT, rhs, start, stop):
    return lambda e: e.matmul(out, lhsT=lhsT, rhs=rhs, start=start, stop=stop)


def build(debug=False, stop_after=99):
    nc = bass.Bass("TRN2", target_bir_lowering=False)

    def din(name, shape):
        return nc.dram_tensor(name, shape, F32, kind="ExternalInput").ap()

    def scratch(name, shape, dt):
        return nc.dram_tensor(name, shape, dt, kind=("ExternalOutput" if debug else "Internal")).ap()

    xfull = din("xfull", [SEQ, D])
    xown = din("xown", [NOWN, D])
    xhalo = din("xhalo", [64, D])
    maskd = din("maskd", [128, 256])
    selv = din("selv", [8, 2])
    triseg = din("triseg", [128, 128])
    lngc_d = din("lngc", [128, 32])
    lnbc_d = din("lnbc", [128, 32])
    w_in0 = din("w_in0", [D, 3080])
    b_f = din("b_f", [8, 1])
    cwl = din("cwl", [128, 12])
    w_out0 = din("w_out0", [D, D])
    w_in1 = din("w_in1", [D, 2048])
    vln_g = din("vln_g", [1, D])
    vln_b = din("vln_b", [1, D])
    wsT_d = din("wsT", [128, 1024])
    b_s = din("b_s", [1, D])
    w_out1 = din("w_out1", [D, D])
    mix_g = din("mix_g", [2, D])
    mix_b = din("mix_b", [2, D])
    ffn_w1 = din("ffn_w1", [2, D, 2 * FH])
    ffn_w2 = din("ffn_w2", [2, FH, D])
    ffn_g = din("ffn_g", [2, D])
    ffn_b = din("ffn_b", [2, D])
    out = nc.dram_tensor("out", [NOWN, D], F32, kind="ExternalOutput").ap()

    KT_s = scratch("KT_s", [H, 70, SEQ], BF16)
    V_s = scratch("V_s", [H, 128, 64, 128], BF16)
    QT_s = scratch("QT_s", [H, 70, NOWN], BF16)
    mixT_s = scratch("mixT_s", [D, NOWN], BF16)
    x1_s = scratch("x1_s", [NOWN, D], F32)
    xT_s = scratch("xT_s", [D, NOWN], BF16)
    aT_s = scratch("aT_s", [FH, NOWN], BF16)
    x2_s = scratch("x2_s", [NOWN, D], F32)
    gT_s = scratch("gT_s", [D, NOWN], BF16)
    x3_s = scratch("x3_s", [NOWN, D], F32)
    f_s = scratch("f_s", [8, SEQ], F32)
    ck_s = scratch("ck_s", [3, 128, 512], BF16)
    cq_s = scratch("cq_s", [3, 128, 256], BF16)

    st = ExitStack()
    with st:
        AW = 45056
        arena_t = st.enter_context(nc.sbuf_tensor("arena", [128, AW], F32))
        bpairs = [st.enter_context(nc.psum_tensor(f"bpair{i}", [128, 1024], F32)) for i in range(4)]
        banks = [bpairs[i // 2][:, (i % 2) * 512:(i % 2 + 1) * 512] for i in range(8)]
        S = Sched(nc)
        bank_ctr = [0]

        def nb():
            i = bank_ctr[0] % 8
            bank_ctr[0] += 1
            return banks[i], ('ps', i)

        P = Arena(arena_t, 0, 4096)
        identf = P.f32(128)
        identb = P.bf16(128)
        maskf = P.f32(256)
        maskb = P.bf16(256)
        onesb = P.bf16(128)
        cw = P.f32(12)
        bft = P.f32(1, 8)
        selt = P.f32(2, 8)
        lnscr = [dict(st=P.f32(12), mv=P.f32(2), sd=P.f32(1), rs=P.f32(1), nb=P.f32(1)) for _ in range(4)]
        lngc = P.f32(32)
        lnbc = P.f32(32)
        epst = P.f32(1)
        ones8p = P.bf16(1024, 8)
        ln_ctr = [0]
        PBASE = P.off

        S.add('pool', lambda e: e.memset(identf, 0.0), writes=['identf'])
        S.add('pool', lambda e: e.affine_select(out=identf, in_=identf, pattern=[[-1, 128]], compare_op=ALU.not_equal,
                                                fill=1.0, base=0, channel_multiplier=1), reads=['identf'], writes=['identf'])
        S.add('dve', lambda e: e.tensor_copy(out=identb, in_=identf), reads=['identf'], writes=['identb'])
        S.add('sp', lambda e: e.dma_start(out=maskf, in_=maskd), writes=['maskf'], dkey='c0')
        S.add('dve', lambda e: e.tensor_copy(out=maskb, in_=maskf), reads=['maskf'], writes=['maskb'])
        S.add('dve', lambda e: e.memset(onesb, 1.0), writes=['onesb'])
        S.add('sp', lambda e: e.dma_start(out=cw, in_=cwl), writes=['cw'], dkey='c1')
        S.add('sp', lambda e: e.dma_start(out=bft, in_=b_f), writes=['bft'], dkey='c2')
        S.add('sp', lambda e: e.dma_start(out=selt, in_=selv), writes=['selt'], dkey='c3')
        S.add('sp', lambda e: e.dma_start(out=lngc, in_=lngc_d), writes=['lngc'], dkey='c16')
        S.add('sp', lambda e: e.dma_start(out=lnbc, in_=lnbc_d), writes=['lnbc'], dkey='c17')

        def new_arena():
            return Arena(arena_t, PBASE, AW)

        def v3(a, d1, d2):
            return a.rearrange("p (a b) -> p a b", a=d1, b=d2)

        def load_w(dst3, src2d, key, kcs, dk):
            for k in range(kcs):
                S.add('pool', lambda e, k=k: e.dma_start(out=dst3[:, k, :], in_=src2d[k * 128:(k + 1) * 128, :]),
                      writes=[(key, k)], dkey=(dk, k % 4))

        def load_bcast(dst, vec_row, key, dk):
            S.add('sp', lambda e: e.dma_start(out=dst, in_=vec_row.broadcast_to([128, D])), writes=[key], dkey=dk)

        def prefetch_ln_weights(nch, w_d, g_row, b_row):
            size = nch * 512 + 2048
            base = AW - size
            T = Arena(arena_t, base, AW)
            Wt = v3(T.bf16(nch * 1024), nch, 1024)
            g_t = T.f32(1024)
            b_t = T.f32(1024)
            todo = []
            for k in range(nch):
                todo.append(lambda k=k: S.add('pool', lambda e: e.dma_start(out=Wt[:, k, :], in_=w_d[k * 128:(k + 1) * 128, :]),
                                              writes=[('Wt', k)], dkey=('w', k % 4)))
            todo.append(lambda: load_bcast(g_t, g_row, 'g_t', 'c6'))
            todo.append(lambda: load_bcast(b_t, b_row, 'b_t', 'c7'))
            return (Wt, g_t, b_t, base, todo)

        GM_LO, GM_HI = 23552, 31744

        def prefetch_gmlp_weights():
            T = Arena(arena_t, GM_LO, GM_HI)
            Wuv = v3(T.bf16(8 * 2048), 8, 2048)
            todo = []
            for k in range(8):
                todo.append(lambda k=k: S.add('pool', lambda e: e.dma_start(out=Wuv[:, k, :], in_=w_in1[k * 128:(k + 1) * 128, :]),
                                              writes=[('Wuv', k)], dkey=('w3', k % 4)))
            return (Wuv, None, None, GM_LO, todo)

        def drain_todo(nxt, n):
            if nxt is None:
                return
            for _ in range(n):
                if nxt[4]:
                    nxt[4].pop(0)()

        def layer_norm(z, zkey, o, okey, g_t, b_t, gkey, bkey, act_norm=False, gb_eng='pool', defer_gb=False):
            sc = lnscr[ln_ctr[0] % 4]
            sk = ('lnscr', ln_ctr[0] % 4)
            ln_ctr[0] += 1
            stt, mv, sd, rs, nbt = sc['st'], sc['mv'], sc['sd'], sc['rs'], sc['nb']
            S.add('dve', [lambda e: e.bn_stats(out=stt[:, 0:6], in_=z[:, 0:512]),
                          lambda e: e.bn_stats(out=stt[:, 6:12], in_=z[:, 512:1024])], reads=[zkey], writes=[(sk, 'st')])
            S.add('dve', lambda e: e.bn_aggr(out=mv, in_=stt), reads=[(sk, 'st')], writes=[sk])
            S.add('act', lambda e: e.activation(out=sd, in_=mv[:, 1:2], func=AF.Sqrt, bias=epst[:, 0:1], scale=1.0),
                  reads=[sk, 'epst'], writes=[(sk, 'sd')])
            S.add('dve', lambda e: e.reciprocal(out=rs, in_=sd), reads=[(sk, 'sd')], writes=[(sk, 'rs')])
            if act_norm:
                S.add('dve', lambda e: e.tensor_scalar(out=nbt, in0=mv[:, 0:1], scalar1=-1.0, scalar2=rs[:, 0:1], op0=ALU.mult, op1=ALU.mult),
                      reads=[sk, (sk, 'rs')], writes=[(sk, 'nb')])
                S.add('act', lambda e: e.activation(out=z, in_=z, func=AF.Identity, bias=nbt[:, 0:1], scale=rs[:, 0:1]),
                      reads=[zkey, (sk, 'rs'), (sk, 'nb')], writes=[zkey])
            else:
                S.add('dve', lambda e: e.tensor_scalar(out=z, in0=z, scalar1=mv[:, 0:1], scalar2=rs[:, 0:1],
                                                       op0=ALU.subtract, op1=ALU.mult),
                      reads=[zkey, sk, (sk, 'rs')], writes=[zkey])
            def gain_bias():
                S.add(gb_eng, lambda e: e.tensor_tensor(out=o, in0=z, in1=g_t, op=ALU.mult), reads=[zkey, gkey], writes=[okey])
                S.add(gb_eng, lambda e: e.tensor_tensor(out=o, in0=o, in1=b_t, op=ALU.add), reads=[okey, bkey], writes=[okey])
            if defer_gb:
                return gain_bias
            gain_bias()
            return None

        S.add('dve', lambda e: e.memset(epst, EPS), writes=['epst'])

        def transposes_to_bf16(src_blk, src_keys, dstT, dst_key_fn, nblk, evac='act', extra_f32=None):
            for kc in range(KC):
                bk, bkk = nb()
                S.add('pe', [lambda e, b=b, kc=kc, bk=bk: e.transpose(out=bk[:, b * 128:(b + 1) * 128],
                                                                      in_=src_blk(b)[:, kc * 128:(kc + 1) * 128],
                                                                      identity=identf) for b in range(nblk)],
                      reads=list(src_keys) + ['identf'], writes=[bkk])
                n = nblk * 128
                if extra_f32 is not None:
                    x32, x32key = extra_f32
                    S.add('act', lambda e, bk=bk, kc=kc: e.copy(out=x32[:, kc, 0:n], in_=bk[:, 0:n]),
                          reads=[bkk], writes=[(x32key, kc)])
                    S.add('dve', lambda e, kc=kc: e.tensor_copy(out=dstT[:, kc, 0:n], in_=x32[:, kc, 0:n]),
                          reads=[(x32key, kc)], writes=[dst_key_fn(kc)])
                elif evac == 'act':
                    S.add('act', lambda e, bk=bk, kc=kc: e.copy(out=dstT[:, kc, 0:n], in_=bk[:, 0:n]),
                          reads=[bkk], writes=[dst_key_fn(kc)])
                else:
                    S.add('dve', lambda e, bk=bk, kc=kc: e.tensor_copy(out=dstT[:, kc, 0:n], in_=bk[:, 0:n]),
                          reads=[bkk], writes=[dst_key_fn(kc)])

        A = new_arena()
        Wkvf = v3(A.bf16(8 * 1032), 8, 1032)
        Wf32 = v3(A.f32(64), 8, 8)
        fT = A.f32(SEQ, 8)
        S1C_BASE = A.off
        xs = [v3(A.f32(4096), 4, 1024) for _ in range(2)]
        xT32 = v3(A.f32(4096), 8, 512)
        xTb = [v3(A.bf16(4096), 8, 512) for _ in range(2)]
        kst = [v3(A.bf16(2048), 4, 512) for _ in range(2)]
        vst = [A.bf16(8 * 4 * 128).rearrange("p (h b c) -> p h b c", h=8, b=4, c=128) for _ in range(2)]

        S.add('sp', lambda e: e.dma_start(out=Wf32, in_=w_in0[:, 1536:1544].rearrange("(k p) n -> p k n", p=128)),
              writes=['Wf32'], dkey='c4')
        S.add('dve', lambda e: e.tensor_copy(out=Wkvf[:, :, 1024:1032], in_=Wf32), reads=['Wf32'], writes=['Wfb'])
        for i in range(2):
            S.add('pool', lambda e, i=i: e.memset(vst[i][:, :, :, 64:128], 1.0), writes=[('vst1', i)])

        NT1 = 16

        def s1_load(t):
            sl = t % 2
            S.add('sp', [lambda e, b=b: e.dma_start(out=xs[sl][:, b, :], in_=xfull[t * 512 + b * 128:t * 512 + (b + 1) * 128, :]) for b in range(4)],
                  writes=[('xs', sl)], dkey=('xs', sl))

        def s1_tr(t):
            sl = t % 2
            transposes_to_bf16(lambda b: xs[sl][:, b, :], [('xs', sl)], xTb[sl], lambda kc: ('xTb', sl, kc), 4)

        def s1_f(t):
            bk, bkk = nb()
            sl = t % 2
            S.add('pe', [mm(bk[0:8, :], Wkvf[:, kc, 1024:1032], xTb[sl][:, kc, :], kc == 0, kc == 7) for kc in range(KC)],
                  reads=[('xTb', sl, kc) for kc in range(KC)] + ['Wfb'], writes=[bkk])
            S.add('dve', lambda e, bk=bk: e.tensor_scalar(out=fT[:, t * 512:(t + 1) * 512], in0=bk[0:8, :], scalar1=bft[:, 0:1], scalar2=None,
                                                          op0=ALU.add), reads=[bkk, 'bft'], writes=[('fT', t)])

        def s1_mm(t):
            sl = t % 2
            xkeys = [('xTb', sl, kc) for kc in range(KC)]
            wkeys = [('Wkvf', k) for k in range(KC)]
            for hp in range(4):
                bk, bkk = nb()
                S.add('pe', [mm(bk[:, :], Wkvf[:, kc, hp * 128:(hp + 1) * 128], xTb[sl][:, kc, :], kc == 0, kc == 7) for kc in range(KC)],
                      reads=xkeys + wkeys, writes=[bkk])
                S.add('dve', lambda e, bk=bk, hp=hp: e.tensor_copy(out=kst[sl][:, hp, :], in_=bk[:, :]), reads=[bkk], writes=[('kst', sl, hp)])
            S.add('sp', [lambda e, h=h: e.dma_start(out=KT_s[h, 0:64, t * 512:(t + 1) * 512],
                                                    in_=kst[sl][(h % 2) * 64:(h % 2 + 1) * 64, h // 2, :]) for h in range(H)],
                  reads=[('kst', sl, hp) for hp in range(4)], writes=[('KTd', h, t) for h in range(H)], dkey=('kst', sl))
            for b in range(4):
                bk, bkk = nb()
                S.add('pe', [mm(bk[:, :], xTb[sl][:, kc, b * 128:(b + 1) * 128], Wkvf[:, kc, 512:1024], kc == 0, kc == 7) for kc in range(KC)],
                      reads=xkeys + wkeys, writes=[bkk])
                S.add('act', lambda e, bk=bk, b=b: e.copy(out=vst[sl][:, :, b, 0:64], in_=bk[:, :].rearrange("p (h c) -> p h c", h=8)),
                      reads=[bkk, ('vst1', sl)], writes=[('vst', sl, b)])
            S.add('sp', lambda e: e.dma_start(out=V_s[:, :, 4 * t:4 * t + 4, :].rearrange("h p b c -> p h b c"), in_=vst[sl]),
                  reads=[('vst', sl, b) for b in range(4)], writes=[('Vd', t)], dkey=('vst', sl))

        s1_load(0)
        Wstg = v3(arena_t[:, AW - 8192:AW], 8, 1024)
        for k in range(8):
            S.add('sp', lambda e, k=k: e.dma_start(out=Wstg[:, k, :], in_=w_in0[k * 128:(k + 1) * 128, 512:1536]), writes=[('wstg', k)], dkey=('wstg', k))
            S.add('act' if k % 2 == 0 else 'dve',
                  (lambda e, k=k: e.copy(out=Wkvf[:, k, 0:1024], in_=Wstg[:, k, :])) if k % 2 == 0 else (lambda e, k=k: e.tensor_copy(out=Wkvf[:, k, 0:1024], in_=Wstg[:, k, :])),
                  reads=[('wstg', k)], writes=[('Wkvf', k)])
        s1_load(1)
        W2TOP = AW - 8192
        assert A.off <= W2TOP, (A.off, W2TOP)
        T2 = Arena(arena_t, W2TOP, AW)
        Wq = v3(T2.bf16(8 * 512), 8, 512)
        Wbch = v3(T2.bf16(8 * 1536), 8, 1536)
        w2todo = []
        for k in range(8):
            w2todo.append(lambda k=k: S.add('pool', lambda e: e.dma_start(out=Wq[:, k, :], in_=w_in0[k * 128:(k + 1) * 128, 0:512]),
                                            writes=[('Wq', k)] + [('wstg', j) for j in range(8)], dkey=('w2', k % 4)))
            w2todo.append(lambda k=k: S.add('pool', lambda e: e.dma_start(out=Wbch[:, k, :], in_=w_in0[k * 128:(k + 1) * 128, 1544:3080]),
                                            writes=[('Wbch', k)] + [('wstg', j) for j in range(8)], dkey=('w2', k % 4)))
        s1_tr(0)
        s1_f(0)
        for t in range(NT1):
            if t == 3:
                S.add('pool', lambda e: e.memset(ones8p, 1.0), writes=['ones8p'])
            if t in (4, 6, 8, 10):
                ci = (t - 4) // 2
                fl = [lambda e, r=r, q=q: e.dma_start(out=KT_s[:, 67 + r, q * 1024:(q + 1) * 1024], in_=ones8p) for r in range(3) for q in range(8)] \
                    + [lambda e, r=r, q=q: e.dma_start(out=QT_s[:, 64 + r, q * 1024:(q + 1) * 1024], in_=ones8p) for r in range(3) for q in range(4)]
                S.add('sp', fl[ci * 9:(ci + 1) * 9], reads=['ones8p'],
                      writes=[('KTc', r) for r in range(67, 70)] + [('QTc', r) for r in range(64, 67)], dkey='ones')
            if t >= 2 and w2todo:
                w2todo.pop(0)()
                if w2todo:
                    w2todo.pop(0)()
            if t + 1 < NT1:
                s1_tr(t + 1)
            if t + 2 < NT1:
                s1_load(t + 2)
            s1_mm(t)
            if t + 1 < NT1:
                s1_f(t + 1)

        S.enabled = stop_after >= 1
        S.barrier()
        S.add('sp', lambda e: e.dma_start(out=f_s, in_=fT), reads=[('fT', t) for t in range(NT1)], writes=['f_s'], dkey='c12')
        S.barrier()
        A = Arena(arena_t, 25600, W2TOP)
        f128 = A.f32(512)
        e128 = A.f32(512)
        ones5 = A.f32(512)
        cs = A.f32(512)
        tri = A.f32(128)
        tot = A.f32(1)
        offs = A.f32(1)
        rt = A.f32(512)
        khi = A.bf16(512)
        kmid = A.bf16(512)
        klo = A.bf16(512)
        qtmp = A.bf16(256)
        qhi = A.bf16(256)
        qmid = A.bf16(256)
        qlo = A.bf16(256)
        sel128 = A.f32(2)
        fkeys = [('fT', t) for t in range(NT1)]
        S.add('sp', lambda e: e.dma_start(out=f128, in_=f_s.rearrange("h (s t) -> (h s) t", s=16)), reads=['f_s'], writes=['f128'], dkey='c13')
        S.add('sp', lambda e: e.dma_start(out=tri, in_=triseg), writes=['tri'], dkey='c14')
        S.add('sp', lambda e: e.dma_start(out=sel128, in_=selv[0:1, :].broadcast_to([128, 2])), writes=['sel128'], dkey='c15')
        S.add('dve', lambda e: e.memset(ones5, 1.0), writes=['ones5'])
        S.add('act', lambda e: e.activation(out=e128, in_=f128, func=AF.Exp, scale=-1.0), reads=['f128'], writes=['e128'])
        S.add('act', lambda e: e.activation(out=f128, in_=e128, func=AF.Ln, bias=1.0, scale=1.0), reads=['e128'], writes=['sp_'])
        S.add('dve', lambda e: e.tensor_tensor_scan(out=cs, data0=ones5, data1=f128, initial=0.0, op0=ALU.mult, op1=ALU.add),
              reads=['sp_', 'ones5'], writes=['cs'])
        S.add('dve', lambda e: e.tensor_copy(out=tot, in_=cs[:, 511:512]), reads=['cs'], writes=['tot'])
        def s1c_part_b():
            S.enabled = stop_after >= 1
            bk, bkk = nb()
            S.add('pe', mm(bk[:, 0:1], tri, tot, True, True), reads=['tri', 'tot'], writes=[bkk])
            S.add('dve', lambda e, bk=bk: e.tensor_copy(out=offs, in_=bk[:, 0:1]), reads=[bkk], writes=['offs'])
            S.add('dve', lambda e: e.tensor_scalar(out=cs, in0=cs, scalar1=offs[:, 0:1], scalar2=None, op0=ALU.add), reads=['cs', 'offs'], writes=['cneg'])

            def split3(src, skey, rtmp, rkey, hi, mid, lo, pfx):
                S.add('dve', lambda e: e.tensor_copy(out=hi, in_=src), reads=[skey], writes=[pfx + 'hi'])
                S.add('dve', lambda e: e.tensor_tensor(out=rtmp, in0=src, in1=hi, op=ALU.subtract), reads=[skey, pfx + 'hi'], writes=[rkey])
                S.add('dve', lambda e: e.tensor_copy(out=mid, in_=rtmp), reads=[rkey], writes=[pfx + 'mid'])
                S.add('dve', lambda e: e.tensor_tensor(out=rtmp, in0=rtmp, in1=mid, op=ALU.subtract), reads=[rkey, pfx + 'mid'], writes=[rkey])
                S.add('dve', lambda e: e.tensor_copy(out=lo, in_=rtmp), reads=[rkey], writes=[pfx + 'lo'])

            split3(cs, 'cneg', rt, 'rt', khi, kmid, klo, 'k')
            qt3 = qtmp.rearrange("p (i t) -> p i t", i=2, t=128)
            for (ksrc, kkey, qdst, qkey) in [(khi, 'khi', qhi, 'qhi'), (kmid, 'kmid', qmid, 'qmid'), (klo, 'klo', qlo, 'qlo')]:
                k4 = ksrc.rearrange("p (i two t) -> p i two t", i=2, two=2, t=128)
                q3 = qdst.rearrange("p (i t) -> p i t", i=2, t=128)
                S.add('dve', lambda e, k4=k4: e.tensor_scalar(out=qt3, in0=k4[:, :, 0, :], scalar1=sel128[:, 0:1], scalar2=None, op0=ALU.mult),
                      reads=[kkey, 'sel128'], writes=['qtmp'])
                S.add('dve', lambda e, k4=k4, q3=q3: e.scalar_tensor_tensor(out=q3, in0=k4[:, :, 1, :], scalar=sel128[:, 1:2], in1=qt3, op0=ALU.mult, op1=ALU.add),
                      reads=[kkey, 'sel128', 'qtmp'], writes=[qkey])
            S.add('sp', [lambda e, r=r, src=src: e.dma_start(out=ck_s[r], in_=src) for r, src in enumerate([khi, kmid, klo])]
                  + [lambda e, r=r, src=src: e.dma_start(out=cq_s[r], in_=src) for r, src in enumerate([qhi, qmid, qlo])],
                  reads=['khi', 'kmid', 'klo', 'qhi', 'qmid', 'qlo'], writes=['ck_s', 'cq_s'], dkey='kc0')
            fns = []
            for r in range(3):
                fns.append(lambda e, r=r: e.dma_start(out=KT_s[:, 64 + r, :], in_=ck_s[r].rearrange("(h s) t -> h (s t)", s=16)))
                fns.append(lambda e, r=r: e.dma_start(out=QT_s[:, 67 + r, :], in_=cq_s[r].rearrange("(h s) t -> h (s t)", s=16)))
            S.add('sp', fns, reads=['ck_s', 'cq_s'],
                  writes=[('KTc', r) for r in range(64, 67)] + [('QTc', r) for r in range(67, 70)], dkey='kc')

            S.enabled = stop_after >= 2

        S.enabled = stop_after >= 2
        A = Arena(arena_t, PBASE, 25600)
        xs2 = [v3(A.f32(4096), 4, 1024) for _ in range(2)]
        xTb2 = [v3(A.bf16(4096), 8, 512) for _ in range(2)]
        qst = [v3(A.bf16(2048), 4, 512) for _ in range(2)]
        zt = A.f32(4 * 4 * 130).rearrange("p (c b t) -> p c b t", c=4, b=4, t=130)
        ctm = [A.f32(512) for _ in range(2)]
        ytm = [A.f32(512) for _ in range(2)]
        btm = [A.f32(512) for _ in range(2)]
        cst = [v3(A.bf16(2048), 4, 512) for _ in range(2)]
        zhalo = v3(A.f32(256), 4, 64)
        xhT = v3(A.bf16(512), 8, 64)
        xh = A.f32(1024, 64)
        chl = A.f32(64)

        wqk = [('Wq', k) for k in range(KC)]
        wbk = [('Wbch', k) for k in range(KC)]
        S.add('sp', lambda e: e.dma_start(out=xh, in_=xhalo), writes=['xh'], dkey='c5')
        bk, bkk = nb()
        S.add('pe', [lambda e, kc=kc, bk=bk: e.transpose(out=bk[:, kc * 64:(kc + 1) * 64], in_=xh[:, kc * 128:(kc + 1) * 128],
                                                         identity=identf[0:64, 0:64]) for kc in range(KC)], reads=['xh', 'identf'], writes=[bkk])
        S.add('act', lambda e, bk=bk: e.copy(out=xhT.rearrange("p k n -> p (k n)"), in_=bk[:, :]), reads=[bkk], writes=['xhT'])
        for cc in range(4):
            bc, bck = nb()
            bh, bhk = nb()
            S.add('pe', [mm(bc[:, 0:64], Wbch[:, kc, 512 + cc * 128:512 + (cc + 1) * 128], xhT[:, kc, :], kc == 0, kc == 7) for kc in range(KC)],
                  reads=['xhT'] + wbk, writes=[bck])
            S.add('pe', [mm(bh[:, 0:64], Wbch[:, kc, 1024 + cc * 128:1024 + (cc + 1) * 128], xhT[:, kc, :], kc == 0, kc == 7) for kc in range(KC)],
                  reads=['xhT'] + wbk, writes=[bhk])
            S.add('act', lambda e, bc=bc: e.copy(out=chl, in_=bc[:, 0:64]), reads=[bck], writes=['chl'])
            S.add('dve', lambda e, bh=bh, cc=cc: e.tensor_tensor(out=zhalo[:, cc, :], in0=bh[:, 0:64], in1=chl, op=ALU.mult),
                  reads=[bhk, 'chl'], writes=[('zhalo', cc)])

        def s2_load(t):
            sl = t % 2
            S.add('sp', lambda e, t=t, sl=sl: e.dma_start(out=xs2[sl], in_=xown[t * 512:(t + 1) * 512, :].rearrange("(b p) d -> p b d", p=128)),
                  writes=[('xs2', sl)], dkey=('xs2', sl))

        def s2_tr(t):
            sl = t % 2
            transposes_to_bf16(lambda b, sl=sl: xs2[sl][:, b, :], [('xs2', sl)], xTb2[sl], lambda kc, sl=sl: ('xTb2', sl, kc), 4)

        def s2_mm(t):
            sl = t % 2
            xkeys = [('xTb2', sl, kc) for kc in range(KC)]
            for hp in range(4):
                bk, bkk = nb()
                S.add('pe', [mm(bk[:, :], Wq[:, kc, hp * 128:(hp + 1) * 128], xTb2[sl][:, kc, :], kc == 0, kc == 7) for kc in range(KC)],
                      reads=xkeys + wqk, writes=[bkk])
                S.add('act', lambda e, bk=bk, hp=hp, sl=sl: e.activation(out=qst[sl][:, hp, :], in_=bk[:, :], func=AF.Copy, scale=0.125),
                      reads=[bkk], writes=[('qst', sl, hp)])
            S.add('sp', [lambda e, h=h, sl=sl, t=t: e.dma_start(out=QT_s[h, 0:64, t * 512:(t + 1) * 512],
                                                              in_=qst[sl][(h % 2) * 64:(h % 2 + 1) * 64, h // 2, :]) for h in range(H)],
                  reads=[('qst', sl, hp) for hp in range(4)], writes=[('QTd', h, t) for h in range(H)], dkey=('qst', sl))
            for cc in range(4):
                bb, bbk = nb()
                bc, bck = nb()
                bh, bhk = nb()
                for (bx, bxk, off) in [(bb, bbk, 0), (bc, bck, 512), (bh, bhk, 1024)]:
                    S.add('pe', [mm(bx[:, :], Wbch[:, kc, off + cc * 128:off + (cc + 1) * 128], xTb2[sl][:, kc, :], kc == 0, kc == 7) for kc in range(KC)],
                          reads=xkeys + wbk, writes=[bxk])
                ci = cc % 2
                S.add('act', lambda e, bc=bc, ci=ci: e.copy(out=ctm[ci], in_=bc[:, :]), reads=[bck], writes=[('ctm', ci)])
                S.add('act', lambda e, bb=bb, ci=ci: e.copy(out=btm[ci], in_=bb[:, :]), reads=[bbk], writes=[('btm', ci)])
                S.add('pool', lambda e, cc=cc, t=t: e.tensor_copy(out=zt[:, cc, :, 0:2], in_=zhalo[:, cc, 8 * t:8 * t + 8].rearrange("p (b j) -> p b j", j=2)),
                      reads=[('zhalo', cc)], writes=[('zth', cc)])
                S.add('dve', lambda e, bh=bh, cc=cc, ci=ci: e.tensor_tensor(out=zt[:, cc, :, 2:130], in0=bh[:, :].rearrange("p (b t) -> p b t", b=4),
                                                                     in1=ctm[ci].rearrange("p (b t) -> p b t", b=4), op=ALU.mult),
                      reads=[bhk, ('ctm', ci)], writes=[('zt', cc)])
                y3 = ytm[ci].rearrange("p (b t) -> p b t", b=4)
                S.add('dve', lambda e, cc=cc, y3=y3: e.tensor_scalar(out=y3, in0=zt[:, cc, :, 0:128], scalar1=cw[:, cc * 3:cc * 3 + 1], scalar2=None, op0=ALU.mult),
                      reads=[('zt', cc), ('zth', cc), 'cw'], writes=[('ytm', ci)])
                for kk in (1, 2):
                    S.add('dve', lambda e, cc=cc, y3=y3, kk=kk: e.scalar_tensor_tensor(out=y3, in0=zt[:, cc, :, kk:kk + 128], scalar=cw[:, cc * 3 + kk:cc * 3 + kk + 1],
                                                                                in1=y3, op0=ALU.mult, op1=ALU.add),
                          reads=[('zt', cc), ('zth', cc), 'cw', ('ytm', ci)], writes=[('ytm', ci)])
                S.add('dve', lambda e, cc=cc, ci=ci, sl=sl: e.tensor_tensor(out=cst[sl][:, cc, :], in0=btm[ci], in1=ytm[ci], op=ALU.mult),
                      reads=[('btm', ci), ('ytm', ci)], writes=[('cst', sl, cc)])
            S.add('sp', lambda e, sl=sl, t=t: e.dma_start(out=mixT_s[512:1024, t * 512:(t + 1) * 512].rearrange("(c p) t -> p c t", p=128), in_=cst[sl]),
                  reads=[('cst', sl, cc) for cc in range(4)], writes=[('mixc', t)], dkey=('cst', sl))

        s2_load(0)
        s2_load(1)
        s2_tr(0)
        for t in range(8):
            if t + 1 < 8:
                s2_tr(t + 1)
            if t + 2 < 8:
                s2_load(t + 2)
            s2_mm(t)
            if t == 1:
                s1c_part_b()

        S.enabled = stop_after >= 3
        if True:
            H0 = Arena(arena_t, 25600, W2TOP)
            KTh0 = H0.bf16(SEQ)
            Vh0 = v3(H0.bf16(64 * 128), 64, 128)
            QTh0 = H0.bf16(NOWN)
            S.add('sp', lambda e: e.dma_start(out=KTh0[0:70, :], in_=KT_s[0]),
                  reads=[('KTd', 0, t) for t in range(NT1)] + [('KTc', r) for r in range(64, 70)], writes=[('KTh', 0)], dkey=('hdK', 0))
            S.add('sp', lambda e: e.dma_start(out=Vh0, in_=V_s[0]), reads=[('Vd', t) for t in range(NT1)], writes=[('Vh', 0)], dkey=('hdV', 0))
            S.add('sp', lambda e: e.dma_start(out=QTh0[0:70, :], in_=QT_s[0]),
                  reads=[('QTd', 0, t) for t in range(8)] + [('QTc', r) for r in range(64, 70)], writes=[('QTh', 0)], dkey=('hdQ', 0))
            S.barrier()
            pre4a = prefetch_ln_weights(8, w_out0, mix_g[0:1, :], mix_b[0:1, :]) if S.enabled else None
            A = Arena(arena_t, PBASE, 25600)
            KTh = [KTh0, A.bf16(SEQ)]
            Vh = [Vh0, v3(A.bf16(64 * 128), 64, 128)]
            QTh = [QTh0, A.bf16(NOWN)]
            pT = [v3(A.bf16(1024), 2, 512) for _ in range(3)]
            rl = [A.f32(512) for _ in range(2)]
            ost = [A.bf16(NOWN) for _ in range(2)]
            ps_s = [(bpairs[i], [('ps', 2 * i), ('ps', 2 * i + 1)]) for i in range(3)]
            ps_o = [(banks[6 + i], ('ps', 6 + i)) for i in range(2)]
            steps = []
            for h in range(H):
                for qt in range(8):
                    i0 = 4 * qt
                    nch = 2 * i0 + 8
                    for j in range(0, nch, 2):
                        steps.append((h, qt, j, nch))

            def load_head(h):
                hs = h % 2
                S.add('sp', lambda e: e.dma_start(out=KTh[hs][0:70, :], in_=KT_s[h]),
                      reads=[('KTd', h, t) for t in range(NT1)] + [('KTc', r) for r in range(64, 70)], writes=[('KTh', hs)], dkey=('hdK', hs))
                S.add('sp', lambda e: e.dma_start(out=Vh[hs], in_=V_s[h]), reads=[('Vd', t) for t in range(NT1)], writes=[('Vh', hs)], dkey=('hdV', hs))
                S.add('sp', lambda e: e.dma_start(out=QTh[hs][0:70, :], in_=QT_s[h]),
                      reads=[('QTd', h, t) for t in range(8)] + [('QTc', r) for r in range(64, 70)], writes=[('QTh', hs)], dkey=('hdQ', hs))

            LA = 2
            n = len(steps)
            for idx in range(n + LA):
                if idx < n:
                    h, qt, j, nch = steps[idx]
                    hs = h % 2
                    i0 = 4 * qt
                    lb = max(i0, j // 2) - i0
                    c0 = lb * 128
                    diag = (j // 2) >= i0
                    bp, bpk = ps_s[idx % 3]
                    fns = []
                    for u in range(2):
                        fns.append(mm(bp[:, u * 512 + c0:(u + 1) * 512], KTh[hs][0:70, (j + u) * 128:(j + u + 1) * 128],
                                      QTh[hs][0:70, qt * 512 + c0:(qt + 1) * 512], True, not diag))
                        if diag:
                            fns.append(mm(bp[:, u * 512 + c0:u * 512 + c0 + 128], identb, maskb[:, u * 128:(u + 1) * 128], False, True))
                    rds = [('KTh', hs), ('QTh', hs)] + (['identb', 'maskb'] if diag else [])
                    S.add('pe', fns, reads=rds, writes=bpk)
                    S.add('act', lambda e, bp=bp, c0=c0, idx=idx: e.activation(out=pT[idx % 3][:, :, c0:512],
                                                                             in_=bp[:, :].rearrange("p (u t) -> p u t", u=2)[:, :, c0:512], func=AF.Exp),
                          reads=bpk, writes=[('pT', idx % 3)])
                k = idx - LA
                if k >= 0:
                    h, qt, j, nch = steps[k]
                    hs = h % 2
                    i0 = 4 * qt
                    lb = max(i0, j // 2) - i0
                    c0 = lb * 128
                    tix = h * 8 + qt
                    bo, bok = ps_o[tix % 2]
                    if qt == 0 and j == 0 and h + 1 < H:
                        load_head(h + 1)
                        drain_todo(pre4a, 2)
                    S.add('pe', [mm(bo[:, c0:512], Vh[hs][:, j + u, :], pT[k % 3][:, u, c0:512], (j + u) == 0, (j + u) == nch - 1) for u in range(2)],
                          reads=[('Vh', hs), ('pT', k % 3)], writes=[bok])
                    if j + 2 == nch:
                        r = rl[tix % 2]
                        S.add('dve', lambda e, bo=bo, r=r: e.reciprocal(out=r[64:128, :], in_=bo[64:128, :]), reads=[bok], writes=[('rl', tix % 2)])
                        S.add('dve', lambda e, bo=bo, r=r, hs=hs, qt=qt: e.tensor_tensor(out=ost[hs][0:64, qt * 512:(qt + 1) * 512], in0=bo[0:64, :],
                                                                                     in1=r[64:128, :], op=ALU.mult),
                              reads=[bok, ('rl', tix % 2)], writes=[('ost', hs)])
                        if qt == 7:
                            S.add('sp', lambda e, h=h, hs=hs: e.dma_start(out=mixT_s[h * 64:(h + 1) * 64, :], in_=ost[hs][0:64, :]),
                                  reads=[('ost', hs)], writes=[('mixa', h)], dkey=('ost', hs))

        def stage_ln(inT_s, in_keys, nch, w_d, res_s, res_keys, g_row, b_row, xo_s, xo_key, xT_o, xTo_key, TT, LAG, pre=None, gset=0, mid_pre=None, reserve_hi=None):
            S.barrier()
            NB = TT // 128
            NTL = NOWN // TT
            mp = mid_pre() if (mid_pre is not None and S.enabled) else None
            if pre is not None and S.enabled:
                Wt, g_t, b_t, top, _todo = pre
                drain_todo(pre, 999)
                A = Arena(arena_t, PBASE, min(top, mp[3] if mp is not None else top, reserve_hi if reserve_hi is not None else top))
            else:
                A = new_arena()
                Wt = v3(A.bf16(nch * 1024), nch, 1024)
                g_t = A.f32(1024)
                b_t = A.f32(1024)
            mT = [v3(A.bf16(nch * TT), nch, TT) for _ in range(2)]
            xr = [v3(A.f32(NB * 1024), NB, 1024) for _ in range(2)]
            NZ = LAG + 2
            zs = [A.f32(1024) for _ in range(NZ)]
            xo = [v3(A.f32(NB * 1024), NB, 1024) for _ in range(2)]
            xTo = [v3(A.bf16(8 * TT), 8, TT) for _ in range(2)]
            if pre is None or not S.enabled:
                load_w(Wt, w_d, 'Wt', nch, 'w')
                load_bcast(g_t, g_row, 'g_t', 'c6')
                load_bcast(b_t, b_row, 'b_t', 'c7')
            wk = [('Wt', k) for k in range(nch)]

            def loads(t):
                sl = t % 2
                S.add('sp', lambda e: e.dma_start(out=mT[sl], in_=inT_s[:, t * TT:(t + 1) * TT].rearrange("(k p) t -> p k t", p=128)),
                      reads=in_keys(t), writes=[('mT', sl)], dkey=('mT', sl))
                S.add('sp', lambda e: e.dma_start(out=xr[sl], in_=res_s[t * TT:(t + 1) * TT, :].rearrange("(b p) d -> p b d", p=128)),
                      reads=res_keys(t), writes=[('xr', sl)], dkey=('xr', sl))

            def emit_tr(t, b, zi):
                sl = t % 2
                for hf in range(2):
                    bk, bkk = nb()
                    S.add('pe', [lambda e, q=q, bk=bk, hf=hf: e.transpose(out=bk[:, q * 128:(q + 1) * 128],
                                                                         in_=zs[zi][:, (hf * 4 + q) * 128:(hf * 4 + q + 1) * 128],
                                                                         identity=identf) for q in range(4)],
                          reads=[('zs', zi), 'identf'], writes=[bkk])
                    for q in range(4):
                        kc = hf * 4 + q
                        S.add('act', lambda e, bk=bk, q=q, kc=kc: e.activation(out=xTo[sl][:, kc, b * 128:(b + 1) * 128], in_=bk[:, q * 128:(q + 1) * 128],
                                                                             func=AF.Identity, bias=lnbc[:, gset * 8 + kc:gset * 8 + kc + 1],
                                                                             scale=lngc[:, gset * 8 + kc:gset * 8 + kc + 1]),
                              reads=[bkk, 'lngc', 'lnbc'], writes=[('xTo', sl, b, kc)])
                if b == NB - 1:
                    S.add('sp', lambda e: e.dma_start(out=xT_o[:, t * TT:(t + 1) * TT].rearrange("(k p) t -> p k t", p=128), in_=xTo[sl]),
                          reads=[('xTo', sl, bb, kc) for bb in range(NB) for kc in range(8)], writes=[(xTo_key, t)], dkey=('xTo', sl))

            pending = []
            deferred_gb = None
            loads(0)
            g = 0
            for t in range(NTL):
                sl = t % 2
                if t + 1 < NTL:
                    loads(t + 1)
                drain_todo(mp, 1)
                for b in range(NB):
                    zi = g % NZ
                    g += 1
                    for half in range(2):
                        bk, bkk = nb()
                        S.add('pe', [mm(bk[:, :], mT[sl][:, c, b * 128:(b + 1) * 128], Wt[:, c, half * 512:(half + 1) * 512], c == 0, c == nch - 1) for c in range(nch)],
                              reads=[('mT', sl)] + wk, writes=[bkk])
                        S.add('dve', lambda e, bk=bk, b=b, half=half, sl=sl, zi=zi: e.scalar_tensor_tensor(
                            out=zs[zi][:, half * 512:(half + 1) * 512], in0=xr[sl][:, b, half * 512:(half + 1) * 512], scalar=ALPHA, in1=bk[:, :],
                            op0=ALU.mult, op1=ALU.add), reads=[bkk, ('xr', sl)], writes=[('zs', zi)])
                    use_dve = (nch == 8) and (g % 2 == 0) and (b != NB - 1)
                    last_blk = (xT_o is None) and (t == NTL - 1)
                    new_gb = layer_norm(zs[zi], ('zs', zi), xo[sl][:, b, :], ('xo', sl, b), g_t, b_t, 'g_t', 'b_t', act_norm=True,
                                        gb_eng=('dve' if (use_dve or last_blk) else 'pool'), defer_gb=use_dve)
                    if deferred_gb is not None:
                        deferred_gb()
                    deferred_gb = new_gb
                    if b == NB - 1:
                        S.add('sp', lambda e, t=t, sl=sl: e.dma_start(out=xo_s[t * TT:(t + 1) * TT, :].rearrange("(b p) d -> p b d", p=128), in_=xo[sl]),
                              reads=[('xo', sl, bb) for bb in range(NB)], writes=[(xo_key, t)], dkey=('xo', sl))
                    if xT_o is not None:
                        pending.append((t, b, zi))
                        if len(pending) > LAG:
                            emit_tr(*pending.pop(0))
            while pending:
                emit_tr(*pending.pop(0))
            drain_todo(mp, 999)
            return mp

        def stage_ffn_in(xT_i, xTi_key, nkeys, w1_d, aT_key, next_pre=None, extra_pre=None):
            S.barrier()
            nxt = next_pre() if (next_pre is not None and S.enabled) else None
            xp = extra_pre() if (extra_pre is not None and S.enabled) else None
            lim = nxt[3] if nxt is not None else AW
            if xp is not None:
                lim = min(lim, xp[3])
            A = Arena(arena_t, PBASE, lim)
            xTa = v3(A.bf16(8 * NOWN), 8, NOWN)
            W1c = [v3(A.bf16(8 * 256), 8, 256) for _ in range(3)]
            sg = [A.bf16(512) for _ in range(2)]
            ast = [A.bf16(1024) for _ in range(4)]
            for t in range(8):
                S.add('sp', lambda e, t=t: e.dma_start(out=xTa[:, :, t * 512:(t + 1) * 512], in_=xT_i[:, t * 512:(t + 1) * 512].rearrange("(k p) t -> p k t", p=128)),
                      reads=[(xTi_key, t), (xTi_key, 2 * t), (xTi_key, 2 * t + 1)], writes=[('xTa', t)], dkey=('xTa', t))
            ctr = 0
            def w1_load(j):
                ws = j % 3
                S.add('pool', lambda e: e.dma_start(out=W1c[ws][:, :, 0:128], in_=w1_d[:, j * 128:(j + 1) * 128].rearrange("(k p) n -> p k n", p=128)),
                      writes=[('W1g', ws)], dkey=('W1g', ws))
                S.add('pool', lambda e: e.dma_start(out=W1c[ws][:, :, 128:256], in_=w1_d[:, FH + j * 128:FH + (j + 1) * 128].rearrange("(k p) n -> p k n", p=128)),
                      writes=[('W1u', ws)], dkey=('W1u', ws))

            w1_load(0)
            w1_load(1)
            for j in range(NJ):
                ws = j % 3
                if j + 2 < NJ:
                    w1_load(j + 2)
                drain_todo(nxt, 2)
                if j >= 11:
                    drain_todo(xp, 1)
                for t in range(8):
                    bg, bgk = nb()
                    bu, buk = nb()
                    S.add('pe', [mm(bg[:, :], W1c[ws][:, kc, 0:128], xTa[:, kc, t * 512:(t + 1) * 512], kc == 0, kc == 7) for kc in range(KC)],
                          reads=[('xTa', t), ('W1g', ws)], writes=[bgk])
                    S.add('pe', [mm(bu[:, :], W1c[ws][:, kc, 128:256], xTa[:, kc, t * 512:(t + 1) * 512], kc == 0, kc == 7) for kc in range(KC)],
                          reads=[('xTa', t), ('W1u', ws)], writes=[buk])
                    si = ctr % 2
                    ai = (ctr // 2) % 4
                    hf = ctr % 2
                    ctr += 1
                    S.add('act', lambda e, bg=bg, si=si: e.activation(out=sg[si], in_=bg[:, :], func=AF.Silu), reads=[bgk], writes=[('sg', si)])
                    S.add('dve', lambda e, bu=bu, si=si, ai=ai, hf=hf: e.tensor_tensor(out=ast[ai][:, hf * 512:(hf + 1) * 512], in0=bu[:, :], in1=sg[si], op=ALU.mult),
                          reads=[buk, ('sg', si)], writes=[('ast', ai, hf)])
                    if hf == 1:
                        S.add('sp', lambda e, j=j, t=t, ai=ai: e.dma_start(out=aT_s[j * 128:(j + 1) * 128, (t - 1) * 512:(t + 1) * 512], in_=ast[ai]),
                              reads=[('ast', ai, 0), ('ast', ai, 1)], writes=[(aT_key, j, t - 1), (aT_key, j, t)], dkey=('ast', ai))

            drain_todo(xp, 999)
            return (nxt, xp)

        def stage_gmlp(xT_i, xTi_key, next_pre=None, pre_w=None):
            S.barrier()
            nxt = next_pre() if (next_pre is not None and S.enabled) else None
            hi_lim = nxt[3] if nxt is not None else AW
            if pre_w is not None and S.enabled:
                Wuv = pre_w[0]
                drain_todo(pre_w, 999)
                A = Arena(arena_t, PBASE, GM_LO)
                A2 = Arena(arena_t, GM_HI, hi_lim)
                g_t = A.f32(1024)
                b_t = A.f32(1024)
                load_bcast(g_t, vln_g[0:1, :], 'vg_t', 'c10')
                load_bcast(b_t, vln_b[0:1, :], 'vb_t', 'c11')
            else:
                A = Arena(arena_t, PBASE, hi_lim)
                A2 = A
                Wuv = v3(A.bf16(8 * 2048), 8, 2048)
                g_t = A.f32(1024)
                b_t = A.f32(1024)
            wsT = v3(A.bf16(1024), 8, 128)
            wsTf = A.f32(1024)
            bsf = A.f32(1024, 1)
            bsr = A.f32(1024, 1)
            bshi = A.bf16(1024, 1)
            bslo = A.bf16(1024, 1)
            xTi = [v3(A.bf16(4096), 8, 512) for _ in range(2)]
            uT = [v3(A.f32(4096), 8, 512) for _ in range(2)]
            vln = [A.bf16(1024) for _ in range(3)]
            gst = [v3(A2.bf16(4096), 8, 512) for _ in range(2)]
            vs = [A2.f32(1024) for _ in range(3)]
            if pre_w is None or not S.enabled:
                load_w(Wuv, w_in1, 'Wuv', 8, 'w')
                load_bcast(g_t, vln_g[0:1, :], 'vg_t', 'c10')
                load_bcast(b_t, vln_b[0:1, :], 'vb_t', 'c11')
            S.add('sp', lambda e: e.dma_start(out=wsTf, in_=wsT_d), writes=['wsTf'], dkey='c8')
            S.add('dve', lambda e: e.tensor_copy(out=wsT.rearrange("p g i -> p (g i)"), in_=wsTf), reads=['wsTf'], writes=['wsT'])
            S.add('dve', lambda e: e.memset(wsT[64:128, :, 0:64], 0.0), reads=['wsT'], writes=['wsT'])
            S.add('sp', lambda e: e.dma_start(out=bsf, in_=b_s), writes=['bsf'], dkey='c9')
            S.add('dve', lambda e: e.tensor_copy(out=bshi, in_=bsf), reads=['bsf'], writes=['bshi'])
            S.add('dve', lambda e: e.tensor_tensor(out=bsr, in0=bsf, in1=bshi, op=ALU.subtract), reads=['bsf', 'bshi'], writes=['bsr'])
            S.add('dve', lambda e: e.tensor_copy(out=bslo, in_=bsr), reads=['bsr'], writes=['bslo'])
            wk = [('Wuv', k) for k in range(KC)]
            def g_load(t):
                sl = t % 2
                S.add('sp', lambda e: e.dma_start(out=xTi[sl], in_=xT_i[:, t * 512:(t + 1) * 512].rearrange("(k p) t -> p k t", p=128)),
                      reads=[(xTi_key, 2 * t), (xTi_key, 2 * t + 1)], writes=[('xTi', sl)], dkey=('xTi', sl))

            def emit_spatial(t, b, vi):
                sl = t % 2
                for hf in range(2):
                    bk, bkk = nb()
                    fns = []
                    for q in range(4):
                        g = hf * 4 + q
                        o_ = bk[:, q * 128:(q + 1) * 128]
                        fns.append(mm(o_, vln[vi][:, g * 128:(g + 1) * 128], wsT[:, g, :], True, False))
                        fns.append(mm(o_, onesb[0:1, 0:128], bshi[0:1, g * 128:(g + 1) * 128], False, False))
                        fns.append(mm(o_, onesb[0:1, 0:128], bslo[0:1, g * 128:(g + 1) * 128], False, True))
                    S.add('pe', fns, reads=[('vln', vi), 'wsT', 'onesb', 'bshi', 'bslo'], writes=[bkk])
                    S.add('dve', lambda e, bk=bk, hf=hf: e.tensor_tensor(out=gst[sl][:, hf * 4:(hf + 1) * 4, b * 128:(b + 1) * 128],
                                                                       in0=bk[:, :].rearrange("p (q t) -> p q t", q=4),
                                                                       in1=uT[sl][:, hf * 4:(hf + 1) * 4, b * 128:(b + 1) * 128], op=ALU.mult),
                          reads=[bkk] + [('uT', sl, hf * 4 + q) for q in range(4)], writes=[('gst', sl, b, hf)])
                if b == 3:
                    S.add('sp', lambda e: e.dma_start(out=gT_s[:, t * 512:(t + 1) * 512].rearrange("(k p) t -> p k t", p=128), in_=gst[sl]),
                          reads=[('gst', sl, bb, hf) for bb in range(4) for hf in range(2)], writes=[('gT', t)], dkey=('gst', sl))

            g_load(0)
            pending = []
            gctr = 0
            LAGG = 2
            for t in range(8):
                sl = t % 2
                if t + 1 < 8:
                    g_load(t + 1)
                drain_todo(nxt, 2)
                for c in range(8):
                    bk, bkk = nb()
                    S.add('pe', [mm(bk[:, :], Wuv[:, kc, c * 128:(c + 1) * 128], xTi[sl][:, kc, :], kc == 0, kc == 7) for kc in range(KC)],
                          reads=[('xTi', sl)] + wk, writes=[bkk])
                    S.add('act', lambda e, bk=bk, c=c, sl=sl: e.activation(out=uT[sl][:, c, :], in_=bk[:, :], func=AF.Gelu), reads=[bkk], writes=[('uT', sl, c)])
                for b in range(4):
                    vi = gctr % 3
                    gctr += 1
                    for half in range(2):
                        bk, bkk = nb()
                        S.add('pe', [mm(bk[:, :], xTi[sl][:, kc, b * 128:(b + 1) * 128], Wuv[:, kc, 1024 + half * 512:1024 + (half + 1) * 512], kc == 0, kc == 7) for kc in range(KC)],
                              reads=[('xTi', sl)] + wk, writes=[bkk])
                        S.add('act', lambda e, bk=bk, half=half, vi=vi: e.activation(out=vs[vi][:, half * 512:(half + 1) * 512], in_=bk[:, :], func=AF.Gelu),
                              reads=[bkk], writes=[('vs', vi)])
                    layer_norm(vs[vi], ('vs', vi), vln[vi], ('vln', vi), g_t, b_t, 'vg_t', 'vb_t')
                    pending.append((t, b, vi))
                    if len(pending) > LAGG:
                        emit_spatial(*pending.pop(0))
            while pending:
                emit_spatial(*pending.pop(0))
            return nxt

        S.enabled = stop_after >= 4
        stage_ln(mixT_s, lambda t: [('mixc', t)] + [('mixa', h) for h in range(H)], 8, w_out0, xown, lambda t: [],
                 mix_g[0:1, :], mix_b[0:1, :], x1_s, 'x1', xT_s, 'x1T', 512, 3, pre=pre4a, gset=0)
        S.enabled = stop_after >= 5
        pre5a, pregm = stage_ffn_in(xT_s, 'x1T', 8, ffn_w1[0], 'aT0',
                                    next_pre=lambda: prefetch_ln_weights(NJ, ffn_w2[0], ffn_g[0:1, :], ffn_b[0:1, :]),
                                    extra_pre=prefetch_gmlp_weights)
        S.enabled = stop_after >= 6
        stage_ln(aT_s, lambda t: [('aT0', j, t // 2) for j in range(NJ)], NJ, ffn_w2[0], x1_s, lambda t: [('x1', t // 2)],
                 ffn_g[0:1, :], ffn_b[0:1, :], x2_s, 'x2', xT_s, 'x2T', 256, 3, pre=pre5a, gset=1, reserve_hi=GM_LO)
        S.enabled = stop_after >= 7
        pre6a = stage_gmlp(xT_s, 'x2T', next_pre=lambda: prefetch_ln_weights(8, w_out1, mix_g[1:2, :], mix_b[1:2, :]), pre_w=pregm)
        S.enabled = stop_after >= 8
        stage_ln(gT_s, lambda t: [('gT', t)], 8, w_out1, x2_s, lambda t: [('x2', 2 * t), ('x2', 2 * t + 1)],
                 mix_g[1:2, :], mix_b[1:2, :], x3_s, 'x3', xT_s, 'x3T', 512, 3, pre=pre6a, gset=2)
        S.enabled = stop_after >= 9
        pre7, _ = stage_ffn_in(xT_s, 'x3T', 8, ffn_w1[1], 'aT1',
                               next_pre=lambda: prefetch_ln_weights(NJ, ffn_w2[1], ffn_g[1:2, :], ffn_b[1:2, :]))
        S.enabled = stop_after >= 10
        stage_ln(aT_s, lambda t: [('aT1', j, t // 2) for j in range(NJ)], NJ, ffn_w2[1], x3_s, lambda t: [('x3', t // 2)],
                 ffn_g[1:2, :], ffn_b[1:2, :], out, 'out', None, None, 256, 2, pre=pre7, gset=3)
        S.enabled = True
        S.emit(st)
        nc._n_sems = S.n_sems
        nc._n_ops = {e: len(S.ops[e]) for e in ENGS}
    return nc


def make_in_maps(inputs):
    f = lambda a: np.ascontiguousarray(np.asarray(a, dtype=np.float32))
    x = f(inputs["x"])
    shared = {
        "w_in0": f(inputs["even_w_in"][0]),
        "b_f": f(inputs["even_b_f"][0]).reshape(8, 1),
        "cwl": f(np.asarray(inputs["even_conv_w"][0]).reshape(3, 4, 128).transpose(2, 1, 0).reshape(128, 12)),
        "w_out0": f(inputs["even_w_out"][0]),
        "w_in1": f(inputs["odd_w_in"][0]),
        "vln_g": f(inputs["odd_v_ln_g"][0]).reshape(1, D),
        "vln_b": f(inputs["odd_v_ln_b"][0]).reshape(1, D),
        "wsT": f(np.asarray(inputs["odd_w_s"][0]).transpose(2, 0, 1).reshape(128, 1024)),
        "b_s": f(inputs["odd_b_s"][0]).reshape(1, D),
        "w_out1": f(inputs["odd_w_out"][0]),
        "mix_g": f(inputs["mix_ln_g"]), "mix_b": f(inputs["mix_ln_b"]),
        "ffn_w1": f(inputs["ffn_w_in"]), "ffn_w2": f(inputs["ffn_w_out"]),
        "ffn_g": f(inputs["ffn_ln_g"]), "ffn_b": f(inputs["ffn_ln_b"]),
    }
    gsets = [inputs["mix_ln_g"][0], inputs["ffn_ln_g"][0], inputs["mix_ln_g"][1], inputs["ffn_ln_g"][1]]
    bsets = [inputs["mix_ln_b"][0], inputs["ffn_ln_b"][0], inputs["mix_ln_b"][1], inputs["ffn_ln_b"][1]]
    shared["lngc"] = f(np.stack([np.asarray(g).reshape(8, 128).T for g in gsets], axis=1).reshape(128, 32))
    shared["lnbc"] = f(np.stack([np.asarray(b).reshape(8, 128).T for b in bsets], axis=1).reshape(128, 32))
    pp = np.arange(128)
    shared["triseg"] = ((pp[:, None] // 16 == pp[None, :] // 16) & (pp[:, None] % 16 < pp[None, :] % 16)).astype(np.float32)
    s_idx = np.arange(128)[:, None]
    t_idx = np.arange(128)[None, :]
    tri = np.where(s_idx <= t_idx, 0.0, NEG).astype(np.float32)
    full0 = np.zeros((128, 128), np.float32)
    fulln = np.full((128, 128), NEG, np.float32)
    in_maps = []
    for c in range(8):
        b, par = c // 2, c % 2
        xb = x[b]
        blocks = xb.reshape(64, 128, D)
        own = np.ascontiguousarray(blocks[par::2]).reshape(NOWN, D)
        halo = np.zeros((32, 2, D), np.float32)
        for i in range(32):
            g = 2 * i + par
            if g > 0:
                halo[i] = xb[g * 128 - 2:g * 128]
        mask = np.concatenate([tri, fulln], axis=1) if par == 0 else np.concatenate([full0, tri], axis=1)
        sel = np.tile(np.array([[-(1.0 - par), -float(par)]], np.float32), (8, 1))
        m = dict(shared)
        m.update({"xfull": xb, "xown": own, "xhalo": halo.reshape(64, D), "maskd": np.ascontiguousarray(mask), "selv": sel})
        in_maps.append(m)
    return in_maps


_NC = None


def kernel(**inputs):
    global _NC
    if _NC is None:
        _NC = build()
    in_maps = make_in_maps(inputs)
    res = run_bass_kernel_spmd(_NC, in_maps, core_ids=list(range(8)))
    outf = np.empty((4, 64, 128, D), np.float32)
    for c in range(8):
        b, par = c // 2, c % 2
        outf[b, par::2] = np.asarray(res.results[c]["out"]).reshape(32, 128, D)
    return outf.reshape(4, SEQ, D)
```
